# Optimizing a Trainium2 kernel written in Bass

```python
import jax, jax.numpy as jnp
from jax import lax
import numpy as np

D_MODEL = 1024
BATCH = 8
SEQ = 8192
DEPTH = 1

HEAD_DIM = 128
N_HEADS_GDN = 4
N_HEADS_FOX = 4
W_GDN = N_HEADS_GDN * HEAD_DIM
W_FOX = N_HEADS_FOX * HEAD_DIM
CONV_WIDTH = 4
CHUNK = 64
Q_BLOCK = 128
EPS = 1e-6
SPLITS = (W_GDN, W_GDN, W_GDN, N_HEADS_GDN, N_HEADS_GDN, W_GDN,
          W_FOX, W_FOX, W_FOX, N_HEADS_FOX, W_FOX,
          D_MODEL, D_MODEL)
D_IN = 4 * W_GDN + 2 * N_HEADS_GDN + 4 * W_FOX + N_HEADS_FOX + 2 * D_MODEL

kernel_name = 'hybrid_gdn_fox_gated_merge_block'


def rmsnorm(x, g):
    xf = x.astype(jnp.float32)
    y = xf * lax.rsqrt(jnp.mean(xf * xf, axis=-1, keepdims=True) + EPS)
    return (y * g.astype(jnp.float32)).astype(x.dtype)


def l2norm(x):
    xf = x.astype(jnp.float32)
    return xf * lax.rsqrt(jnp.sum(xf * xf, axis=-1, keepdims=True) + EPS)


def causal_conv_silu(u, w):
    K = w.shape[0]
    T = u.shape[1]
    up = jnp.pad(u, ((0, 0), (K - 1, 0), (0, 0)))
    y = sum(up[:, i:i + T] * w[i] for i in range(K))
    return jax.nn.silu(y)


def gated_delta_rule(q, k, v, g, beta):
    B, T, H, Dk = q.shape
    Dv = v.shape[-1]
    N = T // CHUNK
    f32 = jnp.float32
    q = q.astype(f32) * (Dk ** -0.5)

    def to_chunks(a):
        a = a.astype(f32).reshape((B, N, CHUNK, H) + a.shape[3:])
        return jnp.swapaxes(a, 2, 3)

    qc, kc, vc = to_chunks(q), to_chunks(k), to_chunks(v)
    gc, bc = to_chunks(g), to_chunks(beta)
    g_cum = jnp.cumsum(gc, axis=-1)
    idx = jnp.arange(CHUNK)
    causal = idx[:, None] >= idx[None, :]
    strict = idx[:, None] > idx[None, :]
    diff = g_cum[..., :, None] - g_cum[..., None, :]
    decay = jnp.where(causal, jnp.exp(jnp.where(causal, diff, 0.0)), 0.0)
    k_beta = kc * bc[..., None]
    v_beta = vc * bc[..., None]
    L = jnp.where(strict, jnp.einsum('bnhcd,bnhsd->bnhcs', k_beta, kc) * decay, 0.0)
    eye = jnp.eye(CHUNK, dtype=f32)
    Tm = lax.linalg.triangular_solve(eye + L, jnp.broadcast_to(eye, L.shape), left_side=True, lower=True)
    u = jnp.einsum('bnhcs,bnhse->bnhce', Tm, v_beta)
    w = jnp.einsum('bnhcs,bnhsd->bnhcd', Tm, k_beta * jnp.exp(g_cum)[..., None])
    qk = jnp.einsum('bnhcd,bnhsd->bnhcs', qc, kc) * decay

    def step(S, inp):
        q_i, k_i, u_i, w_i, qk_i, g_i = inp
        v_new = u_i - jnp.einsum('bhcd,bhde->bhce', w_i, S)
        o = jnp.einsum('bhcd,bhde->bhce', q_i * jnp.exp(g_i)[..., None], S) + jnp.einsum('bhcs,bhse->bhce', qk_i, v_new)
        g_last = g_i[..., -1]
        S = S * jnp.exp(g_last)[..., None, None] + jnp.einsum(
            'bhcd,bhce->bhde', k_i * jnp.exp(g_last[..., None] - g_i)[..., None], v_new)
        return S, o

    xs = tuple(jnp.moveaxis(a, 1, 0) for a in (qc, kc, u, w, qk, g_cum))
    S0 = jnp.zeros((B, H, Dk, Dv), f32)
    _, o = lax.scan(step, S0, xs)
    return o.transpose(1, 0, 3, 2, 4).reshape(B, T, H, Dv)


def forgetting_attention(q, k, v, log_f):
    B, T, H, D = q.shape
    nb = T // Q_BLOCK
    cT = jnp.cumsum(log_f, axis=1).transpose(0, 2, 1)
    qb = q.reshape(B, nb, Q_BLOCK, H, D).transpose(1, 0, 2, 3, 4)
    cb = cT.reshape(B, H, nb, Q_BLOCK).transpose(2, 0, 1, 3)
    kpos = jnp.arange(T)
    scale = D ** -0.5

    def block(args):
        q_i, c_i, i = args
        s = jnp.einsum('bqhd,bkhd->bhqk', q_i, k).astype(jnp.float32) * scale
        s = s + c_i[..., :, None] - cT[:, :, None, :]
        qpos = i * Q_BLOCK + jnp.arange(Q_BLOCK)
        s = jnp.where(qpos[:, None] >= kpos[None, :], s, -jnp.inf)
        p = jax.nn.softmax(s, axis=-1)
        return jnp.einsum('bhqk,bkhd->bqhd', p.astype(v.dtype), v)

    o = lax.map(block, (qb, cb, jnp.arange(nb)))
    return o.transpose(1, 0, 2, 3, 4).reshape(B, T, H, D)


def setup_inputs(seed: int = 0) -> dict:
    key = jax.random.key(seed)
    ks = jax.random.split(key, 18)
    D = D_MODEL
    nrm = jax.random.normal
    x = nrm(ks[0], (BATCH, SEQ, D), jnp.float32)
    c = nrm(ks[1], (BATCH, D), jnp.float32)
    w_ada = 0.1 * nrm(ks[2], (D, 3 * D), jnp.float32) * D ** -0.5
    b_ada = 0.02 * nrm(ks[3], (3 * D,), jnp.float32)
    g_norm = 1.0 + 0.05 * nrm(ks[4], (D,), jnp.float32)
    w_in = nrm(ks[5], (D, D_IN), jnp.float32) * D ** -0.5
    conv_w = nrm(ks[6], (CONV_WIDTH, 3 * W_GDN), jnp.float32) * CONV_WIDTH ** -0.5
    A_log = jnp.log(jax.random.uniform(ks[7], (N_HEADS_GDN,), jnp.float32, 1.0, 16.0))
    dt = jnp.exp(jax.random.uniform(ks[8], (N_HEADS_GDN,), jnp.float32, np.log(1e-3), np.log(1e-1)))
    dt_bias = dt + jnp.log(-jnp.expm1(-dt))
    g_gdn_out = 1.0 + 0.05 * nrm(ks[9], (HEAD_DIM,), jnp.float32)
    g_q_fox = 1.0 + 0.05 * nrm(ks[10], (HEAD_DIM,), jnp.float32)
    g_k_fox = 1.0 + 0.05 * nrm(ks[11], (HEAD_DIM,), jnp.float32)
    b_f = jax.random.uniform(ks[12], (N_HEADS_FOX,), jnp.float32, 1.0, 5.0)
    w_o_gdn = nrm(ks[13], (W_GDN, D), jnp.float32) * W_GDN ** -0.5
    w_o_fox = nrm(ks[14], (W_FOX, D), jnp.float32) * W_FOX ** -0.5
    w_out = nrm(ks[15], (D, D), jnp.float32) * D ** -0.5
    return {'x': x, 'c': c, 'w_ada': w_ada, 'b_ada': b_ada, 'g_norm': g_norm, 'w_in': w_in,
            'conv_w': conv_w, 'A_log': A_log, 'dt_bias': dt_bias, 'g_gdn_out': g_gdn_out,
            'g_q_fox': g_q_fox, 'g_k_fox': g_k_fox, 'b_f': b_f, 'w_o_gdn': w_o_gdn,
            'w_o_fox': w_o_fox, 'w_out': w_out}


def reference(x, c, w_ada, b_ada, g_norm, w_in, conv_w, A_log, dt_bias, g_gdn_out,
              g_q_fox, g_k_fox, b_f, w_o_gdn, w_o_fox, w_out):
    B, T, _ = x.shape
    f32 = jnp.float32
    split_idx = np.cumsum(SPLITS)[:-1].tolist()
    for _layer in range(DEPTH):
        mod = (c @ w_ada + b_ada)[:, None, :]
        shift, scale, gate = jnp.split(mod, 3, axis=-1)
        h = rmsnorm(x, g_norm) * (1.0 + scale) + shift
        proj = h @ w_in
        (qa, ka, va, a_a, b_a, za, qf, kf, vf, f_f, zf, ga, gf) = jnp.split(proj, split_idx, axis=-1)

        qkv = causal_conv_silu(jnp.concatenate([qa, ka, va], axis=-1), conv_w)
        qa, ka, va = jnp.split(qkv, 3, axis=-1)
        qa = l2norm(qa.reshape(B, T, N_HEADS_GDN, HEAD_DIM))
        ka = l2norm(ka.reshape(B, T, N_HEADS_GDN, HEAD_DIM))
        va = va.reshape(B, T, N_HEADS_GDN, HEAD_DIM)
        g_dec = -jnp.exp(A_log.astype(f32)) * jax.nn.softplus(a_a.astype(f32) + dt_bias.astype(f32))
        beta = jax.nn.sigmoid(b_a.astype(f32))
        o_a = gated_delta_rule(qa, ka, va, g_dec, beta).astype(x.dtype)
        o_a = rmsnorm(o_a, g_gdn_out) * jax.nn.silu(za.reshape(B, T, N_HEADS_GDN, HEAD_DIM))
        y_a = o_a.reshape(B, T, W_GDN) @ w_o_gdn

        qf = rmsnorm(qf.reshape(B, T, N_HEADS_FOX, HEAD_DIM), g_q_fox)
        kf = rmsnorm(kf.reshape(B, T, N_HEADS_FOX, HEAD_DIM), g_k_fox)
        vf = vf.reshape(B, T, N_HEADS_FOX, HEAD_DIM)
        log_f = jax.nn.log_sigmoid(f_f.astype(f32) + b_f.astype(f32))
        o_f = forgetting_attention(qf, kf, vf, log_f)
        o_f = o_f * jax.nn.silu(zf.reshape(B, T, N_HEADS_FOX, HEAD_DIM))
        y_f = o_f.reshape(B, T, W_FOX) @ w_o_fox

        merged = jax.nn.sigmoid(ga) * y_a + jax.nn.sigmoid(gf) * y_f
        x = x + gate * (merged @ w_out)
    return x
```

```python
import numpy as np
from contextlib import ExitStack
import concourse.bass as bass
import concourse.mybir as mybir
from concourse.bass_utils import run_bass_kernel_spmd

F32 = mybir.dt.float32
BF16 = mybir.dt.bfloat16
AF = mybir.ActivationFunctionType
ALU = mybir.AluOpType
NDSEM = 6
D = 1024
EPS = 1e-6


class _Stop(Exception):
    pass


class Prog:
    def __init__(self, nc):
        self.nc = nc
        self.ops = []

    limit = None

    excl = ("POT", "PN", "G0", "G1", ("PM", 0), ("PM", 1), ("PO", 0), ("PO", 1))

    def add(self, eng, fn, r=(), w=(), dma=False):
        if self.limit is not None and len(self.ops) >= self.limit:
            raise _Stop()
        w = tuple(w) + tuple(k for k in r if k in self.excl and k not in w)
        self.ops.append((eng, fn, tuple(r), tuple(w), dma))

    def pe(self, fn, r=(), w=()):
        self.add("pe", fn, r, w)

    def dma(self, q, fn, r=(), w=()):
        self.add(q, fn, r, w, dma=True)

    def emit(self):
        nc = self.nc
        ops = self.ops
        n = len(ops)
        last_w = {}
        readers = {}
        deps = [None] * n
        for i, (eng, fn, r, w, dma) in enumerate(ops):
            d = set()
            for k in r:
                if k in last_w:
                    d.add(last_w[k])
            for k in w:
                if k in last_w:
                    d.add(last_w[k])
                d.update(readers.get(k, ()))
            d.discard(i)
            deps[i] = d
            for k in r:
                readers.setdefault(k, []).append(i)
            for k in w:
                last_w[k] = i
                readers[k] = []
        needs_sig = [False] * n
        for i in range(n):
            for j in deps[i]:
                if ops[j][4]:
                    continue
                if ops[j][0] == "pe" and ops[i][0] == "pe" and not ops[i][4]:
                    continue
                needs_sig[j] = True
        engs = sorted(set(o[0] for o in ops))
        cnt = {e: 0 for e in engs}
        sig = [None] * n
        dma_cnt = {e: 0 for e in engs}
        dma_prev = {}
        pre_wait = [None] * n
        for i, (eng, fn, r, w, dma) in enumerate(ops):
            if dma:
                slot = dma_cnt[eng] % NDSEM
                dma_cnt[eng] += 1
                prev = dma_prev.get((eng, slot), 0)
                if prev:
                    pre_wait[i] = (("d", eng, slot), prev)
                dma_prev[(eng, slot)] = prev + 16
                sig[i] = (("d", eng, slot), prev + 16)
            elif needs_sig[i]:
                cnt[eng] += 1
                sig[i] = (("c", eng), cnt[eng])
        semkeys = set(s_[0] for s_ in sig if s_ is not None)
        stack = ExitStack()
        sems = {}
        for k in sorted(semkeys, key=str):
            sems[k] = stack.enter_context(nc.semaphore("s_" + "_".join(map(str, k))))
        seen = {e: {} for e in engs}
        plan = {e: [] for e in engs}
        last_dma = {}
        for i, (eng, fn, r, w, dma) in enumerate(ops):
            waits = {}
            if pre_wait[i] is not None:
                k, v = pre_wait[i]
                waits[k] = max(waits.get(k, 0), v)
            for j in deps[i]:
                if sig[j] is None:
                    continue
                k, v = sig[j]
                if k == ("c", "pe") and eng == "pe" and not dma:
                    continue
                waits[k] = max(waits.get(k, 0), v)
            wl = []
            for k, v in waits.items():
                if seen[eng].get(k, 0) >= v:
                    continue
                seen[eng][k] = v
                wl.append((k, v))
            plan[eng].append((wl, fn, sig[i], dma))
            if dma:
                last_dma[sig[i][0]] = sig[i][1]
        self.n_ops = n
        block = stack.enter_context(nc.Block())

        def mk(eng):
            def body(e):
                for wl, fn, sg, dma in plan[eng]:
                    for k, v in wl:
                        e.wait_ge(sems[k], v)
                    ins = fn(e)
                    if sg is not None:
                        ins.then_inc(sems[sg[0]], 16 if dma else 1)
                for k, v in last_dma.items():
                    if k[1] == eng:
                        e.wait_ge(sems[k], v)
            return body

        reg = {"pe": block.tensor, "act": block.scalar, "dve": block.vector,
               "pool": block.gpsimd, "sp": block.sync}
        for eng in engs:
            reg[eng](mk(eng))
        stack.close()


def build(T, stage=9, dq='pool', PC='pool', limit=None):
    NT = T // 512
    NBLK = T // 128
    nc = bass.Bass("TRN2", target_bir_lowering=False)
    di = lambda name, shape, dt=F32: nc.dram_tensor(name, list(shape), dt, kind="ExternalInput").ap()
    x_d = di("x", [T, D])
    cT_d = di("cT", [128, 8])
    wada_d = di("w_ada", [D, 3 * D])
    badaT_d = di("b_adaT", [128, 16])
    bgate_d = di("b_gate", [128, D])
    gnT_d = di("g_normT", [128, 8])
    wAk_d = di("wAk", [D, 512])
    wAv_d = di("wAv", [D, 512])
    wAf_d = di("wAf", [D, 4])
    wB_d = di("wB", [D, 5120])
    wab_d = di("wab", [D, 8])
    convT_d = di("conv_wT", [128, 12, 4])
    alog_d = di("alog_bc", [128, 16])
    dtb_d = di("dtb_bc", [128, 16])
    bf_d = di("bf_bc", [128, 16])
    ggdn_d = di("g_gdn", [128, 1])
    gq_d = di("g_q", [128, 1])
    gk_d = di("g_k", [128, 1])
    gqr_d = di("g_q_row", [1, 128])
    gkr_d = di("g_k_row", [1, 128])
    wog_d = di("w_o_gdn", [512, D])
    wof_d = di("w_o_fox", [512, D])
    wout_d = di("w_out", [D, D])
    ident_d = di("ident", [128, 128])
    tri_d = di("tri", [128, 128])
    ml0_d = di("ml0", [128, 128])
    ml1_d = di("ml1", [128, 128])
    ml2_d = di("ml2", [128, 128])
    sel_d = di("sel4", [4, 512])
    out_d = nc.dram_tensor("out", [T, D], F32, kind="ExternalOutput").ap()
    wBbf_d = nc.dram_tensor("wB_bf", [D, 5120], BF16).ap()
    KT_d = nc.dram_tensor("KT_s", [4, 128, T], BF16).ap()
    V1_d = nc.dram_tensor("V1_s", [NBLK, 128, 516], BF16).ap()

    st = ExitStack()
    sb = lambda name, shape, dt=F32: st.enter_context(nc.sbuf_tensor(name, list(shape), dt))
    ps = lambda name, shape, dt=F32: st.enter_context(nc.psum_tensor(name, list(shape), dt))
    P = Prog(nc)
    P.limit = limit

    def dma(q, out, in_, r, w):
        if q == "pool":
            q = dq
        P.dma(q, lambda e: e.dma_start(out=out, in_=in_), r, w)

    def mm(out, lhsT, rhs, start, stop, r, w):
        P.pe(lambda e: e.matmul(out, lhsT=lhsT, rhs=rhs, start=start, stop=stop), r, w)

    def tr(out, in_, idn, r, w):
        P.pe(lambda e: e.transpose(out=out, in_=in_, identity=idn), r, w)

    def act(out, in_, func, r, w, scale=1.0, bias=None, accum=None, eng="act"):
        kw = {}
        if bias is not None:
            kw["bias"] = bias
        if accum is not None:
            kw["accum_out"] = accum
        P.add(eng, lambda e: e.activation(out=out, in_=in_, func=func, scale=scale, **kw), r, w)

    def ts(eng, out, in0, s1, s2, op0, op1, r, w):
        if s2 is None:
            P.add(eng, lambda e: e.tensor_scalar(out=out, in0=in0, scalar1=s1, scalar2=None, op0=op0), r, w)
        else:
            P.add(eng, lambda e: e.tensor_scalar(out=out, in0=in0, scalar1=s1, scalar2=s2, op0=op0, op1=op1), r, w)

    def tt(eng, out, in0, in1, op, r, w):
        P.add(eng, lambda e: e.tensor_tensor(out=out, in0=in0, in1=in1, op=op), r, w)

    def stt(eng, out, in0, scalar, in1, op0, op1, r, w):
        P.add(eng, lambda e: e.scalar_tensor_tensor(out=out, in0=in0, scalar=scalar, in1=in1, op0=op0, op1=op1), r, w)

    def cp(eng, out, in_, r, w):
        if eng == "act":
            P.add(eng, lambda e: e.activation(out=out, in_=in_, func=AF.Copy), r, w)
        else:
            P.add(eng, lambda e: e.tensor_copy(out=out, in_=in_), r, w)

    def mset(eng, ap, val, w):
        P.add(eng, lambda e: e.memset(ap, val), (), w)

    def recip(out, in_, r, w):
        P.add("dve", lambda e: e.reciprocal(out=out, in_=in_), r, w)

    identf = sb("identf", [128, 128]); trif = sb("trif", [128, 128]); ml0f = sb("ml0f", [128, 128]); ml1f = sb("ml1f", [128, 128]); ml2f = sb("ml2f", [128, 128])
    onesf = sb("onesf", [128, 128]); identb = sb("identb", [128, 128], BF16); onesb = sb("onesb", [128, 128], BF16)
    sel4 = sb("sel4s", [4, 512])
    onec = sb("onec", [128, 1]); epsc = sb("epsc", [128, 1])
    cT = sb("cTs", [128, 8]); badaT = sb("badaTs", [128, 16]); gnT = sb("gnTs", [128, 8])
    aT = sb("aT", [128, 8]); shT = sb("shT", [128, 8]); modT = sb("modT", [128, 16])
    convw = sb("convw", [128, 12, 4]); negA = sb("negA", [128, 16]); dtb = sb("dtb", [128, 16]); bfb = sb("bfb", [128, 16])
    ggdn = sb("ggdn", [128, 1]); gq = sb("gqs", [128, 1]); gk = sb("gks", [128, 1]); gqs = sb("gqsc", [128, 1])
    grow = sb("grow", [1, 256]); grow2 = sb("grow2", [1, 256]); gmx = sb("gmx", [1, 2]); mb = sb("mb", [128, 1]); negm = sb("negm", [128, 1])
    woutb = sb("woutb", [128, 8, D], BF16); wogb = sb("wogb", [128, 4, D], BF16); wofb = sb("wofb", [128, 4, D], BF16)
    wabb = sb("wabb", [128, 8, 8], BF16); wAfb = sb("wAfb", [128, 8, 4], BF16)
    wg = [sb("wg%d" % i, [128, 8, 512], BF16) for i in range(2)]
    stg = [sb("stg%d" % i, [128, 1024]) for i in range(2)]
    stgb = [sb("stgb%d" % i, [128, 512], BF16) for i in range(2)]
    hnb = [sb("hn%d" % i, [128, D], BF16) for i in range(2)]
    ssq = sb("ssq", [128, 4]); rstd = sb("rstd", [128, 4])
    hT = sb("hT", [128, 8, 512], BF16)
    sq = [sb("sq%d" % i, [128, 512], BF16) for i in range(2)]; rst = [sb("rst%d" % i, [128, 512]) for i in range(2)]
    ktile = [sb("ktile%d" % i, [128, 512], BF16) for i in range(2)]
    v1blk = [sb("v1blk%d" % i, [128, 4, 129], BF16) for i in range(2)]
    LOGF = sb("LOGF", [128, NBLK * 4]); CTOK = sb("CTOK", [128, NBLK * 4]); BASE = sb("BASE", [128, NBLK * 4])
    TOTS = sb("TOTS", [128, NBLK * 4]); REFM = sb("REFM", [128, NBLK * 4])
    tmp16 = sb("tmp16", [128, 16]); tmp16b = sb("tmp16b", [128, 16])
    U = sb("U", [128, 4, 515]); halo = sb("halo", [128, 12, 3])
    cBv = lambda k: U[:, k // 4, (k % 4) * 128:(k % 4 + 1) * 128]
    cBk = lambda k: ("U", k // 4)
    gatev = lambda n2: U[:, 2 + n2, 0:512]
    gatek = lambda n2: ("U", 2 + n2)
    cacc = [sb("cacc%d" % i, [128, 512]) for i in range(2)]
    etmp = [sb("etmp%d" % i, [128, 512]) for i in range(2)]
    QKV = sb("QKV", [128, 12, 512], BF16)
    SZ = sb("SZ", [128, 8, 512], BF16); QT = sb("QT", [128, 4, 512], BF16)
    SG = sb("SG", [128, 4, 512], BF16)
    oaT = sb("oaT", [128, 4, 512], BF16); ofT = sb("ofT", [128, 4, 512], BF16)
    mT = sb("mT", [128, 8, 512], BF16); ma = sb("ma", [128, 512])
    gtok = sb("gtok", [128, 16]); btok = sb("btok", [128, 16]); nbeta = sb("nbeta", [128, 16])
    gc = sb("gc", [128, 16]); gsum = sb("gsum", [128, 16]); egc = sb("egc", [128, 16]); egl = sb("egl", [128, 16])
    ekd = sb("ekd", [128, 16]); bek = sb("bek", [128, 16]); gcrow4 = sb("gcrow4", [4, 512])
    H4 = range(4)
    mkf = lambda nm: [sb("%s_%d" % (nm, h), [128, 128]) for h in H4]
    mkb = lambda nm: [sb("%s_%d" % (nm, h), [128, 128], BF16) for h in H4]
    t1 = mkf("t1"); t2 = mkf("t2"); x1 = mkf("x1"); pt1 = mkf("pt1"); Uf = mkf("Uf"); Of = mkf("Of")
    Xb = [[sb("Xb%d_%d" % (i, h), [128, 128], BF16) for i in range(2)] for h in H4]
    Yb = [[sb("Yb%d_%d" % (i, h), [128, 128], BF16) for i in range(2)] for h in H4]
    Rb = mkb("Rb"); Tb = mkb("Tb"); Wb = mkb("Wb"); X1b = mkb("X1b"); X2b = mkb("X2b"); PTm = mkb("PTm")
    Vb = mkb("Vb"); Kb = mkb("Kb"); Kd = mkb("Kd"); WT = mkb("WT"); vnew = mkb("vnew"); onb = mkb("onb")
    ms = [sb("ms_%d" % h, [128, 1]) for h in H4]; junk128 = sb("junk128", [128, 128], BF16)
    Sf = [sb("Sf%d" % h, [128, 128]) for h in range(4)]
    Sb = [sb("Sb%d" % h, [128, 128], BF16) for h in range(4)]
    crt = sb("crt", [128, 4]); oden = sb("oden", [128, 4]); Zz = sb("Zz", [128, 4, 4]); crx = sb("crx", [36, 512], BF16); crh = sb("crh", [4, 512])
    ones36 = sb("ones36", [36, 128], BF16)
    BT = sb("BT", [128, NBLK])
    kst = [sb("kst%d" % i, [128, 512], BF16) for i in range(2)]
    vst = [sb("vst%d" % i, [128, 4, 516], BF16) for i in range(2)]
    ptt = [sb("ptt%d" % i, [128, 512], BF16) for i in range(2)]
    trib = sb("trib", [128, 128], BF16)
    rl = sb("rl", [128, 1]); onf = sb("onf", [128, 128], BF16)

    POT = ps("POT", [128, 512])
    PM = [ps("PM%d" % i, [128, 512]) for i in range(2)]
    PN = ps("PN", [128, 512])
    G0 = ps("G0", [128, 4, 128]); G1 = ps("G1", [128, 4, 128])
    PO = [ps("PO%d" % i, [128, 4, 128]) for i in range(2)]

    PM = [PM[0][:], PM[1][:]] + [X_[:].rearrange("p a b -> p (a b)") for X_ in (G0, G1, PO[0], PO[1])]
    PMK = [("PM", 0), ("PM", 1), "G0", "G1", ("PO", 0), ("PO", 1)]
    POa = [X_[:].rearrange("p a b -> p (a b)")[:, 0:129] for X_ in (PO[0], PO[1], G0, G1)]
    GB = [G0, G1, PO[0], PO[1]]
    GBk = ["G0", "G1", ("PO", 0), ("PO", 1)]
    POk = [("PO", 0), ("PO", 1), "G0", "G1"]
    for name, dst, src in (("identf", identf, ident_d), ("trif", trif, tri_d), ("ml0f", ml0f, ml0_d), ("ml1f", ml1f, ml1_d), ("ml2f", ml2f, ml2_d),
                           ("sel4", sel4, sel_d), ("cT", cT, cT_d), ("badaT", badaT, badaT_d), ("gnT", gnT, gnT_d),
                           ("convw", convw, convT_d), ("negA", negA, alog_d), ("dtb", dtb, dtb_d), ("bfb", bfb, bf_d),
                           ("ggdn", ggdn, ggdn_d), ("gq", gq, gq_d), ("gk", gk, gk_d)):
        dma("sp", dst[:], src, [], [name])
    for n2 in range(2):
        dma("sp", gatev(n2), bgate_d[:, n2 * 512:(n2 + 1) * 512], [], [gatek(n2)])
    dma("sp", grow[:, 0:128], gqr_d, [], ["grow"])
    dma("sp", grow[:, 128:256], gkr_d, [], ["grow"])
    mset("dve", onesf[:], 1.0, ["onesf"]); mset("dve", onesb[:], 1.0, ["onesb"]); mset("dve", onec[:], 1.0, ["onec"])
    mset("dve", epsc[:], EPS, ["epsc"]); mset("dve", ones36[:], 1.0, ["ones36"]); mset("dve", crx[:], 0.0, ["crx"])
    mset("dve", Zz[:], 0.0, ["Zz"]); mset("dve", halo[:], 0.0, ["halo"])
    for i in range(2):
        mset("dve", v1blk[i][:], 1.0, [("v1blk", i)])
    for h in range(4):
        mset("dve", Sf[h][:], 0.0, [("Sf", h)]); mset("dve", Sb[h][:], 0.0, [("Sb", h)])
    cp("dve", identb[:], identf[:], ["identf"], ["identb"])
    cp("dve", trib[:], trif[:], ["trif"], ["trib"])
    ts("dve", convw[:], convw[:], 0.5, None, ALU.mult, None, ["convw"], ["convw"])
    ts("dve", ggdn[:], ggdn[:], 0.5, None, ALU.mult, None, ["ggdn"], ["ggdn"])
    act(negA[:], negA[:], AF.Exp, ["negA"], ["negA"])
    ts("dve", negA[:], negA[:], -1.0, None, ALU.mult, None, ["negA"], ["negA"])
    ts("dve", gqs[:], gq[:], 128.0 ** -0.5, None, ALU.mult, None, ["gq"], ["gqs"])
    ts("dve", grow2[:], grow[:], -1.0, None, ALU.mult, None, ["grow"], ["grow2"])
    tt("dve", grow[:], grow[:], grow2[:], ALU.max, ["grow", "grow2"], ["grow"])
    P.add("dve", lambda e: e.reduce_max(out=gmx[:, 0:1], in_=grow[:, 0:128], axis=mybir.AxisListType.X), ["grow"], ["gmx"])
    P.add("dve", lambda e: e.reduce_max(out=gmx[:, 1:2], in_=grow[:, 128:256], axis=mybir.AxisListType.X), ["grow"], ["gmx"])
    ts("dve", gmx[:, 0:1], gmx[:, 0:1], gmx[:, 1:2], -(128.0 ** 0.5), ALU.mult, ALU.mult, ["gmx"], ["gmx"])
    mm(PN[:, 0:1], onesf[0:1, :], gmx[:, 0:1], True, True, ["onesf", "gmx"], ["PN"])
    cp("dve", negm[:], PN[:, 0:1], ["PN"], ["negm"])

    for j in range(16):
        s_ = stg[j % 2]
        dma("sp", s_[:].rearrange("p (k n) -> p k n", k=8), wada_d[:, j * 128:(j + 1) * 128].rearrange("(k p) n -> p k n", p=128),
            [], [("stg", j % 2)])
        for k in range(8):
            mm(PN[:, 16 + j:17 + j], s_[:, k * 128:(k + 1) * 128], cT[:, k:k + 1], k == 0, k == 7,
               [("stg", j % 2), "cT"], ["PN"])
    for j in range(16):
        tt("dve", modT[:, j:j + 1], PN[:, 16 + j:17 + j], badaT[:, j:j + 1], ALU.add, ["PN", "badaT"], ["modT"])
    cp("dve", shT[:], modT[:, 0:8], ["modT"], ["shT"])
    stt("dve", aT[:], modT[:, 8:16], 1.0, gnT[:], ALU.add, ALU.mult, ["modT", "gnT"], ["aT"])
    for k in range(8):
        ts("dve", cBv(k), onesf[:], cT[:, k:k + 1], None, ALU.mult, None, ["onesf", "cT"], [cBk(k)])
    for k in range(8):
        s_ = stg[k % 2]
        dma("sp", s_[:], wada_d[k * 128:(k + 1) * 128, 2048:3072], [], [("stg", k % 2)])
        for n2 in range(2):
            mm(PM[n2][:], cBv(k), s_[:, n2 * 512:(n2 + 1) * 512], k == 0, k == 7, [cBk(k), ("stg", k % 2)], [("PM", n2)])
    for n2 in range(2):
        tt("dve", gatev(n2), PM[n2][:], gatev(n2), ALU.add, [("PM", n2), gatek(n2)], [gatek(n2)])
    si = 0
    for k in range(8):
        s_ = stg[si % 2]
        dma("sp", s_[:], wout_d[k * 128:(k + 1) * 128, :], [], [("stg", si % 2)])
        for n2 in range(2):
            stt("dve", woutb[:, k, n2 * 512:(n2 + 1) * 512], s_[:, n2 * 512:(n2 + 1) * 512], 0.5, gatev(n2), ALU.mult, ALU.mult,
                [("stg", si % 2), gatek(n2)], ["woutb"])
        si += 1
    for (wd, wb_, nm) in ((wog_d, wogb, "wogb"), (wof_d, wofb, "wofb")):
        for k in range(4):
            s_ = stg[si % 2]
            dma("sp", s_[:], wd[k * 128:(k + 1) * 128, :], [], [("stg", si % 2)])
            act(wb_[:, k, :], s_[:], AF.Identity, [("stg", si % 2)], [nm], scale=(0.5 if nm == "wofb" else 1.0))
            si += 1
    s_ = stg[si % 2]
    dma("sp", s_[:, 0:64].rearrange("p (k n) -> p k n", k=8), wab_d.rearrange("(k p) n -> p k n", p=128), [], [("stg", si % 2)])
    cp("dve", wabb[:], s_[:, 0:64].rearrange("p (k n) -> p k n", k=8), [("stg", si % 2)], ["wabb"])
    si += 1
    s_ = stg[si % 2]
    dma("sp", s_[:, 0:32].rearrange("p (k n) -> p k n", k=8), wAf_d.rearrange("(k p) n -> p k n", p=128), [], [("stg", si % 2)])
    cp("dve", wAfb[:], s_[:, 0:32].rearrange("p (k n) -> p k n", k=8), [("stg", si % 2)], ["wAfb"])
    si += 1
    for k in range(8):
        for g in range(5):
            s_ = stg[si % 2]
            dma("sp", s_[:], wB_d[k * 128:(k + 1) * 128, g * 1024:(g + 1) * 1024], [], [("stg", si % 2)])
            for hh in range(2):
                cp("act" if hh else "dve", stgb[hh][:], s_[:, hh * 512:(hh + 1) * 512], [("stg", si % 2)], [("stgb", hh)])
                dma("pool", wBbf_d[k * 128:(k + 1) * 128, g * 1024 + hh * 512:g * 1024 + (hh + 1) * 512], stgb[hh][:],
                    [("stgb", hh)], ["wBbf"])
            si += 1
    for (wd, wi) in ((wAk_d, 0), (wAv_d, 1)):
        for k in range(8):
            s_ = stg[si % 2]
            dma("sp", s_[:, 0:512], wd[k * 128:(k + 1) * 128, :], [], [("stg", si % 2)])
            cp("act", wg[wi][:, k, :], s_[:, 0:512], [("stg", si % 2)], [("wg", wi)])
            si += 1

    if stage == 0:
        P.emit(); st.close(); return nc, P.n_ops
    xi = [0]

    def make_hT(tt_):
        t0 = tt_ * 512
        for b in range(4):
            s_i = xi[0] % 2
            xi[0] += 1
            xb = stg[s_i]
            hb = hnb[b % 2]
            kh = ("hn", b % 2)
            dma("sp", xb[:], x_d[t0 + b * 128:t0 + (b + 1) * 128, :], [], [("stg", s_i)])
            mset("dve", ssq[:, b:b + 1], 0.0, [("ssq", b)])
            act(hb[:], xb[:], AF.Square, [("stg", s_i), ("ssq", b)], [kh, ("ssq", b)], scale=1.0 / 32, accum=ssq[:, b:b + 1])
            act(rstd[:, b:b + 1], ssq[:, b:b + 1], AF.Ln, [("ssq", b), "epsc"], [("rstd", b)], bias=epsc[:])
            act(rstd[:, b:b + 1], rstd[:, b:b + 1], AF.Exp, [("rstd", b)], [("rstd", b)], scale=-0.5)
            ts("dve", hb[:], xb[:], rstd[:, b:b + 1], None, ALU.mult, None, [("stg", s_i), ("rstd", b)], [kh])
            for k4 in range(2):
                i = next_pm()
                for k in range(k4 * 4, k4 * 4 + 4):
                    mm(PM[i][:, (k % 4) * 128:(k % 4 + 1) * 128], hb[:, k * 128:(k + 1) * 128], identb[:], True, True,
                       [kh, "identb"], [PMK[i]])
                for k in range(k4 * 4, k4 * 4 + 4):
                    src = PM[i][:, (k % 4) * 128:(k % 4 + 1) * 128]
                    if k % 2 == 0:
                        act(hT[:, k, b * 128:(b + 1) * 128], src, AF.Identity, [PMK[i], "aT", "shT"], [("hT", k)],
                            scale=aT[:, k:k + 1], bias=shT[:, k:k + 1])
                    else:
                        ts("dve", hT[:, k, b * 128:(b + 1) * 128], src, aT[:, k:k + 1], shT[:, k:k + 1], ALU.mult, ALU.add,
                           [PMK[i], "aT", "shT"], [("hT", k)])

    pmi = [0]
    nrm_n = [0]

    def norm_gen(src_fn, out_ap, out_keys, mean_scale, scalar, scalar_keys, post=None):
        src_ap, src_keys = src_fn()
        j = nrm_n[0] % 2
        nrm_n[0] += 1
        act(sq[j][:], src_ap, AF.Square, src_keys, [("sq", j)])
        yield
        i = next_pm()
        mm(PM[i][:], onesb[:], sq[j][:], True, True, ["onesb", ("sq", j)], [PMK[i]])
        yield
        act(rst[j][:], PM[i][:], AF.Ln, [PMK[i], "epsc"], [("rst", j)], scale=mean_scale, bias=epsc[:])
        yield
        act(rst[j][:], rst[j][:], AF.Exp, [("rst", j)], [("rst", j)], scale=-0.5)
        yield
        stt("dve", out_ap, src_ap, scalar, rst[j][:], ALU.mult, ALU.mult, src_keys + scalar_keys + [("rst", j)], out_keys)
        if post is not None:
            post()

    def run_staggered(pend, maxlive=2):
        gens = []
        while gens or pend:
            if pend and len(gens) < maxlive:
                gens.append(pend.pop(0))
            for g_ in list(gens):
                try:
                    next(g_)
                except StopIteration:
                    gens.remove(g_)

    def next_pm():
        i = pmi[0] % 6
        pmi[0] += 1
        return i

    for tt_ in range(NT):
        t0 = tt_ * 512
        make_hT(tt_)
        def k_src(h):
            def f():
                i = next_pm()
                for k in range(8):
                    mm(PM[i][:], wg[0][:, k, h * 128:(h + 1) * 128], hT[:, k, :], k == 0, k == 7, [("wg", 0), ("hT", k)], [PMK[i]])
                return PM[i][:], [PMK[i]]
            return f

        def k_post(h):
            return lambda: dma("pool", KT_d[h, :, t0:t0 + 512], ktile[h % 2][:], [("ktile", h % 2)], ["KT_d"])

        run_staggered([norm_gen(k_src(h), ktile[h % 2][:], [("ktile", h % 2)], 1.0 / 128, gk[:], ["gk"], post=k_post(h))
                       for h in range(4)])
        for b in range(4):
            i = next_pm()
            vi = b % 2
            for k in range(8):
                mm(PM[i][:], hT[:, k, b * 128:(b + 1) * 128], wg[1][:, k, :], k == 0, k == 7, [("wg", 1), ("hT", k)], [PMK[i]])
            cp("act", v1blk[vi][:, :, 0:128], PM[i][:].rearrange("p (h d) -> p h d", h=4), [PMK[i]], [("v1blk", vi)])
            dma("pool", V1_d[tt_ * 4 + b], v1blk[vi][:].rearrange("p h d -> p (h d)"), [("v1blk", vi)], ["V1_d"])
            for k in range(8):
                mm(PN[:, b * 4:(b + 1) * 4], hT[:, k, b * 128:(b + 1) * 128], wAfb[:, k, :], k == 0, k == 7,
                   ["wAfb", ("hT", k)], ["PN"])
        tt("dve", tmp16[:], PN[:, 0:16], bfb[:], ALU.add, ["PN", "bfb"], ["tmp16"])
        act(tmp16[:], tmp16[:], AF.Exp, ["tmp16"], ["tmp16"], scale=-1.0)
        act(tmp16[:], tmp16[:], AF.Ln, ["tmp16", "onec"], ["tmp16"], bias=onec[:])
        ts("dve", LOGF[:, tt_ * 16:(tt_ + 1) * 16], tmp16[:], -1.0, None, ALU.mult, None, ["tmp16"], ["LOGF"])

    NB4 = NBLK * 4
    for c0 in range(0, NB4, 512):
        c1 = min(NB4, c0 + 512)
        mm(PM[0][:, 0:c1 - c0], trif[:], LOGF[:, c0:c1], True, True, ["trif", "LOGF"], [("PM", 0)])
        mm(PM[1][:, 0:c1 - c0], onesf[:], LOGF[:, c0:c1], True, True, ["onesf", "LOGF"], [("PM", 1)])
        cp("dve", CTOK[:, c0:c1], PM[0][:, 0:c1 - c0], [("PM", 0)], ["CTOK"])
        cp("dve", TOTS[:, c0:c1], PM[1][:, 0:c1 - c0], [("PM", 1)], ["TOTS"])
    mset("dve", BASE[:, 0:4], 0.0, ["BASE"])
    for bl in range(1, NBLK):
        tt("dve", BASE[:, bl * 4:(bl + 1) * 4], BASE[:, (bl - 1) * 4:bl * 4], TOTS[:, (bl - 1) * 4:bl * 4], ALU.add,
           ["BASE", "TOTS"], ["BASE"])
    tt("dve", CTOK[:], CTOK[:], BASE[:], ALU.add, ["CTOK", "BASE"], ["CTOK"])
    ts("dve", REFM[:], BASE[:], negm[:], None, ALU.add, None, ["BASE", "negm"], ["REFM"])

    if stage == 1:
        P.emit(); st.close(); return nc, P.n_ops
    wgi = [0]
    kvi = [0]
    G0k = lambda s: "G0"
    G1k = lambda s: "G1"

    WORDER = [0, 1, 2, 3, 5, 4, 6, 7, 8, 9]
    wpending = {}

    def _issue_w(seq):
        i = seq % 2
        g = WORDER[seq % 10]
        dma("sp", wg[i][:], wBbf_d[:, g * 512:(g + 1) * 512].rearrange("(k p) n -> p k n", p=128),
            ["wBbf"], [("wg", i)])
        wpending[seq] = i

    def load_wgroup(g):
        seq = wgi[0]
        wgi[0] += 1
        assert WORDER[seq % 10] == g
        if seq not in wpending:
            _issue_w(seq)
        i = wpending.pop(seq)
        if seq + 1 < 10 * NT:
            wnext[0] = seq + 1
        return i

    wnext = [None]

    def prefetch_w():
        if wnext[0] is not None and wnext[0] not in wpending:
            _issue_w(wnext[0])
        wnext[0] = None

    def proj_chunk(wi, c):
        i = next_pm()
        for k in range(8):
            mm(PM[i][:], wg[wi][:, k, c * 128:(c + 1) * 128], hT[:, k, :], k == 0, k == 7, [("wg", wi), ("hT", k)], [PMK[i]])
        if c == 0:
            prefetch_w()
        return i

    def silu_from(src_ap, src_keys, out_ap, out_keys, ei):
        e_ = etmp[ei]
        act(e_[:], src_ap, AF.Exp, src_keys, [("etmp", ei)], scale=-1.0)
        ts("dve", e_[:], e_[:], 1.0, None, ALU.add, None, [("etmp", ei)], [("etmp", ei)])
        recip(e_[:], e_[:], [("etmp", ei)], [("etmp", ei)])
        tt("dve", out_ap, src_ap, e_[:], ALU.mult, src_keys + [("etmp", ei)], out_keys)

    def passB_tile(tt_):
        t0 = tt_ * 512
        make_hT(tt_)
        for b in range(4):
            for k in range(8):
                mm(PN[:, 32 + b * 8:40 + b * 8], hT[:, k, b * 128:(b + 1) * 128], wabb[:, k, :], k == 0, k == 7,
                   ["wabb", ("hT", k)], ["PN"])
        if stage == 2:
            raise _Stop()
        def conv_chunk(g, c, wi):
            ch = g * 4 + c
            i = proj_chunk(wi, c)
            cp("dve", U[:, c, 0:3], halo[:, ch, :], ["halo"], [("U", c)])
            cp("act", U[:, c, 3:515], PM[i][:], [PMK[i]], [("U", c)])
            yield
            ci = ch % 2
            acc = cacc[ci]
            ts("dve", acc[:], U[:, c, 0:512], convw[:, ch, 0:1], None, ALU.mult, None, [("U", c), "convw"], [("cacc", ci)])
            for tap in range(1, 4):
                stt("dve", acc[:], U[:, c, tap:tap + 512], convw[:, ch, tap:tap + 1], acc[:], ALU.mult, ALU.add,
                    [("U", c), "convw", ("cacc", ci)], [("cacc", ci)])
            cp(PC, halo[:, ch, :], U[:, c, 512:515], [("U", c)], ["halo"])
            act(etmp[ci][:], acc[:], AF.Tanh, [("cacc", ci)], [("etmp", ci)])
            yield
            yield
            stt("dve", QKV[:, ch, :], etmp[ci][:], 1.0, acc[:], ALU.add, ALU.mult, [("etmp", ci), ("cacc", ci)], [("QKV", ch)])

        for g in range(3):
            wi = load_wgroup(g)
            gens = []
            pend = [conv_chunk(g, c, wi) for c in range(4)]
            while gens or pend:
                if pend:
                    gens.append(pend.pop(0))
                for g_ in list(gens):
                    try:
                        next(g_)
                    except StopIteration:
                        gens.remove(g_)
        if stage == 3:
            raise _Stop()
        for g in (3, 5):
            wi = load_wgroup(g)
            for c in range(4):
                i = proj_chunk(wi, c)
                zc = (0 if g == 3 else 4) + c
                ei = c % 2
                act(etmp[ei][:], PM[i][:], AF.Tanh, [PMK[i]], [("etmp", ei)], scale=0.5)
                stt("dve", SZ[:, zc, :], etmp[ei][:], 1.0, PM[i][:], ALU.add, ALU.mult, [("etmp", ei), PMK[i]], [("SZ", zc)])
        pab = PN[:, 32:64].rearrange("p (b c) -> p b c", b=4)
        tt("dve", tmp16[:].rearrange("p (b h) -> p b h", b=4), pab[:, :, 0:4], dtb[:].rearrange("p (b h) -> p b h", b=4), ALU.add,
           ["PN", "dtb"], ["tmp16"])
        act(tmp16[:], tmp16[:], AF.Exp, ["tmp16"], ["tmp16"])
        act(tmp16[:], tmp16[:], AF.Ln, ["tmp16", "onec"], ["tmp16"], bias=onec[:])
        tt("dve", gtok[:], tmp16[:], negA[:], ALU.mult, ["tmp16", "negA"], ["gtok"])
        act(tmp16b[:].rearrange("p (b h) -> p b h", b=4), pab[:, :, 4:8], AF.Exp, ["PN"], ["tmp16b"], scale=-1.0)
        ts("dve", tmp16b[:], tmp16b[:], 1.0, None, ALU.add, None, ["tmp16b"], ["tmp16b"])
        recip(btok[:], tmp16b[:], ["tmp16b"], ["btok"])
        ts("dve", nbeta[:], btok[:], -1.0, None, ALU.mult, None, ["btok"], ["nbeta"])
        mm(PN[:, 64:80], trif[:], gtok[:], True, True, ["trif", "gtok"], ["PN"])
        mm(PN[:, 80:96], onesf[:], gtok[:], True, True, ["onesf", "gtok"], ["PN"])
        cp("dve", gc[:], PN[:, 64:80], ["PN"], ["gc"])
        cp("dve", gsum[:], PN[:, 80:96], ["PN"], ["gsum"])
        for b in range(4):
            mm(PM[0][0:4, b * 128:(b + 1) * 128], gtok[:, b * 4:(b + 1) * 4], trif[:], True, True, ["trif", "gtok"], [("PM", 0)])
        cp("dve", gcrow4[:], PM[0][0:4, :], [("PM", 0)], ["gcrow4"])
        act(egc[:], gc[:], AF.Exp, ["gc"], ["egc"])
        act(egl[:], gsum[:], AF.Exp, ["gsum"], ["egl"])
        tt("dve", ekd[:], gsum[:], gc[:], ALU.subtract, ["gsum", "gc"], ["ekd"])
        act(ekd[:], ekd[:], AF.Exp, ["ekd"], ["ekd"])
        tt("dve", bek[:], btok[:], egc[:], ALU.mult, ["btok", "egc"], ["bek"])

        wi4 = load_wgroup(4)

        def qk_src(ch):
            return lambda: (QKV[:, ch, :], [("QKV", ch)])

        def qf_src(c):
            def f():
                i = proj_chunk(wi4, c)
                return PM[i][:], [PMK[i]]
            return f

        run_staggered([norm_gen(qk_src(ch), QKV[:, ch, :], [("QKV", ch)], 1.0, (128.0 ** -0.5) if ch < 4 else 1.0, [])
                       for ch in range(8)]
                      + [norm_gen(qf_src(c), QT[:, c, :], [("QT", c)], 1.0 / 128, gqs[:], ["gqs"]) for c in range(4)])

        if stage == 4:
            raise _Stop()
        def gdn_head(h):
            GBh = GB[h]
            gk = GBk[h]
            kK, kQ, kV = ("QKV", 4 + h), ("QKV", h), ("QKV", 8 + h)
            K_ = lambda nm: (nm, h)
            for b in range(4):
                bc = slice(b * 128, (b + 1) * 128)
                r_ = b * 4 + h
                Ka = QKV[:, 4 + h, bc]
                Qa = QKV[:, h, bc]
                Va = QKV[:, 8 + h, bc]
                mm(GBh[:, 0, :], sel4[:, h * 128:(h + 1) * 128], gcrow4[:, bc], True, True, ["sel4", "gcrow4"], [gk])
                yield
                ts("dve", t1[h][:], GBh[:, 0, :], gc[:, r_:r_ + 1], 0.0, ALU.subtract, ALU.min, [gk, "gc"], [K_("t1")])
                ts("dve", t2[h][:], GBh[:, 0, :], gc[:, r_:r_ + 1], 0.0, ALU.subtract, ALU.max, [gk, "gc"], [K_("t2")])
                yield
                act(t1[h][:], t1[h][:], AF.Exp, [K_("t1")], [K_("t1")])
                act(t2[h][:], t2[h][:], AF.Exp, [K_("t2")], [K_("t2")], scale=-1.0)
                mm(GBh[:, 1, :], Ka, Ka, True, True, [kK], [gk])
                mm(GBh[:, 2, :], Ka, Qa, True, True, [kK, kQ], [gk])
                yield
                stt("dve", x1[h][:], GBh[:, 1, :], nbeta[:, r_:r_ + 1], t2[h][:], ALU.mult, ALU.mult, [gk, "nbeta", K_("t2")], [K_("x1")])
                tt("dve", pt1[h][:], GBh[:, 2, :], t1[h][:], ALU.mult, [gk, K_("t1")], [K_("pt1")])
                yield
                tt(PC, Xb[h][0][:], x1[h][:], ml0f[:], ALU.mult, [K_("x1"), "ml0f"], [("Xb", 0, h)])
                tt(PC, X1b[h][:], x1[h][:], ml1f[:], ALU.mult, [K_("x1"), "ml1f"], [K_("X1b")])
                tt(PC, X2b[h][:], x1[h][:], ml2f[:], ALU.mult, [K_("x1"), "ml2f"], [K_("X2b")])
                tt(PC, PTm[h][:], pt1[h][:], trif[:], ALU.mult, [K_("pt1"), "trif"], [K_("PTm")])
                yield
                mm(GBh[:, 3, :], Xb[h][0][:], identb[:], True, True, [("Xb", 0, h), "identb"], [gk])
                yield
                cp("act", Yb[h][0][:], GBh[:, 3, :], [gk], [("Yb", 0, h)])
                tt("dve", Rb[h][:], GBh[:, 3, :], identb[:], ALU.add, [gk, "identb"], [K_("Rb")])
                yield
                cur = 0
                for lv in range(1, 5):
                    nx = 1 - cur
                    if lv <= 3:
                        mm(GBh[:, 0, :], Xb[h][cur][:], Yb[h][cur][:], True, True, [("Xb", cur, h), ("Yb", cur, h)], [gk])
                    mm(GBh[:, 1, :], Yb[h][cur][:], Xb[h][cur][:], True, True, [("Xb", cur, h), ("Yb", cur, h)], [gk])
                    yield
                    if lv <= 3:
                        cp("dve", Yb[h][nx][:], GBh[:, 0, :], [gk], [("Yb", nx, h)])
                    cp("act", Xb[h][nx][:], GBh[:, 1, :], [gk], [("Xb", nx, h)])
                    yield
                    mm(GBh[:, 2, :], Xb[h][nx][:], Rb[h][:], True, True, [("Xb", nx, h), K_("Rb")], [gk])
                    yield
                    tt("dve", Rb[h][:], GBh[:, 2, :], Rb[h][:], ALU.add, [K_("Rb"), gk], [K_("Rb")])
                    yield
                    cur = nx
                for Xl, kX in ((X1b[h], K_("X1b")), (X2b[h], K_("X2b"))):
                    mm(GBh[:, 0, :], Rb[h][:], identb[:], True, True, [K_("Rb"), "identb"], [gk])
                    mm(GBh[:, 1, :], Xl[:], Rb[h][:], True, True, [kX, K_("Rb")], [gk])
                    yield
                    cp("act", Tb[h][:], GBh[:, 0, :], [gk], [K_("Tb")])
                    cp("dve", Wb[h][:], GBh[:, 1, :], [gk], [K_("Wb")])
                    yield
                    mm(GBh[:, 2, :], Tb[h][:], Wb[h][:], True, True, [K_("Tb"), K_("Wb")], [gk])
                    yield
                    tt("dve", Rb[h][:], GBh[:, 2, :], Rb[h][:], ALU.add, [K_("Rb"), gk], [K_("Rb")])
                    yield
                mm(GBh[:, 0, :], Va, identb[:], True, True, [kV, "identb"], [gk])
                mm(GBh[:, 1, :], Ka, identb[:], True, True, [kK, "identb"], [gk])
                yield
                ts("dve", Vb[h][:], GBh[:, 0, :], btok[:, r_:r_ + 1], None, ALU.mult, None, [gk, "btok"], [K_("Vb")])
                ts("dve", Kb[h][:], GBh[:, 1, :], bek[:, r_:r_ + 1], None, ALU.mult, None, [gk, "bek"], [K_("Kb")])
                act(Kd[h][:], GBh[:, 1, :], AF.Identity, [gk, "ekd"], [K_("Kd")], scale=ekd[:, r_:r_ + 1])
                yield
                mm(GBh[:, 2, :], Rb[h][:], Vb[h][:], True, True, [K_("Rb"), K_("Vb")], [gk])
                mm(GBh[:, 3, :], Kb[h][:], Rb[h][:], True, True, [K_("Rb"), K_("Kb")], [gk])
                yield
                cp("act", Uf[h][:], GBh[:, 2, :], [gk], [K_("Uf")])
                cp("dve", WT[h][:], GBh[:, 3, :], [gk], [K_("WT")])
                yield
                mm(GBh[:, 0, :], WT[h][:], Sb[h][:], True, True, [K_("WT"), ("Sb", h)], [gk])
                mm(GBh[:, 1, :], Qa, Sb[h][:], True, True, [kQ, ("Sb", h)], [gk])
                yield
                stt("dve", vnew[h][:], GBh[:, 0, :], -1.0, Uf[h][:], ALU.mult, ALU.add, [K_("Uf"), gk], [K_("vnew")])
                act(Of[h][:], GBh[:, 1, :], AF.Identity, [gk, "egc"], [K_("Of")], scale=egc[:, r_:r_ + 1])
                yield
                mm(GBh[:, 2, :], PTm[h][:], vnew[h][:], True, True, [K_("PTm"), K_("vnew")], [gk])
                mm(GBh[:, 3, :], Kd[h][:], vnew[h][:], True, True, [K_("Kd"), K_("vnew")], [gk])
                yield
                tt("dve", Of[h][:], GBh[:, 2, :], Of[h][:], ALU.add, [K_("Of"), gk], [K_("Of")])
                ts("dve", Sf[h][:], Sf[h][:], egl[:, r_:r_ + 1], None, ALU.mult, None, [("Sf", h), "egl"], [("Sf", h)])
                tt("dve", Sf[h][:], GBh[:, 3, :], Sf[h][:], ALU.add, [("Sf", h), gk], [("Sf", h)])
                yield
                cp("act", Sb[h][:], Sf[h][:], [("Sf", h)], [("Sb", h)])
                mset("dve", ms[h][:], 0.0, [K_("ms")])
                act(junk128[:], Of[h][:], AF.Square, [K_("Of"), K_("ms")], ["junk128", K_("ms")], scale=128.0 ** -0.5, accum=ms[h][:])
                yield
                act(ms[h][:], ms[h][:], AF.Ln, [K_("ms"), "epsc"], [K_("ms")], bias=epsc[:])
                act(ms[h][:], ms[h][:], AF.Exp, [K_("ms")], [K_("ms")], scale=-0.5)
                yield
                ts("dve", onb[h][:], Of[h][:], ms[h][:], None, ALU.mult, None, [K_("Of"), K_("ms")], [K_("onb")])
                yield
                mm(GBh[:, 0, :], onb[h][:], identb[:], True, True, [K_("onb"), "identb"], [gk])
                yield
                stt("dve", oaT[:, h, bc], GBh[:, 0, :], ggdn[:], SZ[:, h, bc], ALU.mult, ALU.mult,
                    [gk, "ggdn", ("SZ", h)], [("oaT", h)])
                yield

        nkb = 4 * tt_ + 4
        noff = 4 * tt_

        def attn_gen():
            for h in range(4):
                for b in range(4):
                    col = (4 * tt_ + b) * 4 + h
                    ts("dve", Zz[:, b, b:b + 1], CTOK[:, col:col + 1], BASE[:, 16 * tt_ + h:16 * tt_ + h + 1], None, ALU.subtract, None,
                       ["CTOK", "BASE"], ["Zz"])
                for b in range(4):
                    mm(PN[0:4, b * 128:(b + 1) * 128], Zz[:, b, :], identf[:], True, True, ["Zz", "identf"], ["PN"])
                cp("dve", crx[0:4, :], PN[0:4, :], ["PN"], ["crx"])
                tt("dve", crh[:], PN[0:4, :], crx[0:4, :], ALU.subtract, ["PN", "crx"], ["crh"])
                cp("dve", crx[32:36, :], crh[:], ["crh"], ["crx"])
                P.add("dve", lambda e, h=h: e.tensor_scalar(
                    out=BT[:, 0:nkb], in0=CTOK[:, 0:nkb * 4].rearrange("p (k h) -> p k h", h=4)[:, :, h], scalar1=-1.0,
                    scalar2=REFM[:, 16 * tt_ + h:16 * tt_ + h + 1], op0=ALU.mult, op1=ALU.add), ["CTOK", "REFM"], ["BT"])
                yield
                if noff:
                    mm(PN[:], ones36[:], crx[:], True, True, ["ones36", "crx"], ["PN"])
                    act(ma[:], PN[:], AF.Exp, ["PN"], ["ma"])
                    yield
                blocks = []
                for kt in range(tt_ + 1):
                    for j in range(4):
                        blocks.append((kt, j))
                kvslot = {}

                def issue_kv(kt):
                    if kt > tt_ or kt in kvslot:
                        return
                    si_ = kt % 2
                    kvslot[kt] = si_
                    dma("sp", kst[si_][:], KT_d[h, :, kt * 512:(kt + 1) * 512], ["KT_d"], [("kst", si_)])
                    dma("sp", vst[si_][:], V1_d[kt * 4:(kt + 1) * 4].rearrange("b p n -> p b n"), ["V1_d"], [("vst", si_)])

                def emit_scores(bi):
                    kt, j = blocks[bi]
                    si_ = kvslot[kt]
                    diag = kt == tt_
                    c0 = j * 128 if diag else 0
                    i = bi % 2
                    mm(PM[i][:, c0:512], kst[si_][:, j * 128:(j + 1) * 128], QT[:, h, c0:512], True, not diag,
                       [("kst", si_), ("QT", h)], [PMK[i]])
                    if diag:
                        mm(PM[i][:, c0:512], ones36[:], crx[:, c0:512], False, True, ["ones36", "crx"], [PMK[i]])

                def emit_rest(bi):
                    kt, j = blocks[bi]
                    si_ = kvslot[kt]
                    kb = kt * 4 + j
                    diag = kt == tt_
                    c0 = j * 128 if diag else 0
                    i = bi % 2
                    pi = bi % 2
                    act(ptt[pi][:, c0:512], PM[i][:, c0:512], AF.Exp, [PMK[i], "BT"], [("ptt", pi)], bias=BT[:, kb:kb + 1])
                    if diag:
                        tt(PC, ptt[pi][:, c0:c0 + 128], ptt[pi][:, c0:c0 + 128], trib[:], ALU.mult, [("ptt", pi), "trib"], [("ptt", pi)])
                        st_, sp_ = (j == 0), (j == 3)
                    else:
                        st_, sp_ = (kb == 0), (kb == noff - 1)
                    mm(POT[:, c0:512], vst[si_][:, j, h * 129:h * 129 + 128], ptt[pi][:, c0:512], st_, sp_,
                       [("ptt", pi), ("vst", si_)], ["POT"])
                    mm(PN[:, c0:512], onesb[:], ptt[pi][:, c0:512], st_, sp_, [("ptt", pi), "onesb"], ["PN"])
                    if (not diag) and kb == noff - 1:
                        tt("dve", cacc[0][:], POT[:], ma[:], ALU.mult, ["POT", "ma"], [("cacc", 0)])
                        tt("dve", cacc[1][:], PN[:], ma[:], ALU.mult, ["PN", "ma"], [("cacc", 1)])

                issue_kv(0)
                issue_kv(1)
                emit_scores(0)
                for bi in range(len(blocks)):
                    if bi + 1 < len(blocks):
                        emit_scores(bi + 1)
                    emit_rest(bi)
                    if blocks[bi][1] == 3:
                        issue_kv(blocks[bi][0] + 2)
                    yield
                if noff:
                    tt("dve", etmp[1][:], PN[:], cacc[1][:], ALU.add, ["PN", ("cacc", 1)], [("etmp", 1)])
                    tt("dve", etmp[0][:], POT[:], cacc[0][:], ALU.add, ["POT", ("cacc", 0)], [("etmp", 0)])
                else:
                    cp("dve", etmp[1][:], PN[:], ["PN"], [("etmp", 1)])
                    cp("dve", etmp[0][:], POT[:], ["POT"], [("etmp", 0)])
                yield
                recip(etmp[1][:], etmp[1][:], [("etmp", 1)], [("etmp", 1)])
                tt("dve", etmp[0][:], etmp[0][:], etmp[1][:], ALU.mult, [("etmp", 0), ("etmp", 1)], [("etmp", 0)])
                tt("dve", ofT[:, h, :], etmp[0][:], SZ[:, 4 + h, :], ALU.mult, [("etmp", 0), ("SZ", 4 + h)], [("ofT", h)])
                yield

        gens = [gdn_head(h) for h in range(4)]
        ag = attn_gen()
        ratio = (12.0 + 16.0 * (tt_ + 1)) / 176.0
        accr = 0.0
        ag_live = True
        while gens:
            for g_ in list(gens):
                try:
                    next(g_)
                except StopIteration:
                    gens.remove(g_)
            accr += ratio
            while ag_live and accr >= 1.0:
                accr -= 1.0
                try:
                    next(ag)
                except StopIteration:
                    ag_live = False
        while ag_live:
            try:
                next(ag)
            except StopIteration:
                ag_live = False

        if stage == 6:
            raise _Stop()
        for g4 in range(4):
            wi = load_wgroup(6 + g4)
            for c in range(4):
                i = proj_chunk(wi, c)
                act(SG[:, c, :], PM[i][:], AF.Tanh, [PMK[i]], [("SG", c)], scale=0.5)
            for c2 in range(2):
                n = g4 * 2 + c2
                i = next_pm()
                for h in range(4):
                    mm(PM[i][:], wogb[:, h, n * 128:(n + 1) * 128], oaT[:, h, :], h == 0, h == 3, ["wogb", ("oaT", h)], [PMK[i]])
                stt("dve", ma[:], SG[:, c2, :], 1.0, PM[i][:], ALU.add, ALU.mult, [PMK[i], ("SG", c2)], ["ma"])
                i = next_pm()
                for h in range(4):
                    mm(PM[i][:], wofb[:, h, n * 128:(n + 1) * 128], ofT[:, h, :], h == 0, h == 3, ["wofb", ("ofT", h)], [PMK[i]])
                stt("dve", etmp[0][:], SG[:, 2 + c2, :], 1.0, PM[i][:], ALU.add, ALU.mult, [PMK[i], ("SG", 2 + c2)], [("etmp", 0)])
                tt("dve", mT[:, n, :], etmp[0][:], ma[:], ALU.add, [("etmp", 0), "ma"], [("mT", n)])
        if stage == 7:
            raise _Stop()
        for b in range(4):
            s_i = xi[0] % 2
            xi[0] += 1
            xb = stg[s_i]
            dma("sp", xb[:], x_d[t0 + b * 128:t0 + (b + 1) * 128, :], [], [("stg", s_i)])
            for n2 in range(2):
                i = next_pm()
                for k in range(8):
                    mm(PM[i][:], mT[:, k, b * 128:(b + 1) * 128], woutb[:, k, n2 * 512:(n2 + 1) * 512], k == 0, k == 7,
                       [("mT", k), "woutb"], [PMK[i]])
                tt("dve", xb[:, n2 * 512:(n2 + 1) * 512], PM[i][:], xb[:, n2 * 512:(n2 + 1) * 512], ALU.add,
                   [PMK[i], ("stg", s_i)], [("stg", s_i)])
            dma("pool", out_d[t0 + b * 128:t0 + (b + 1) * 128, :], xb[:], [("stg", s_i)], [("out", tt_, b)])

    try:
        for tt_ in range(NT):
            passB_tile(tt_)
    except _Stop:
        pass
    P.emit()
    st.close()
    return nc, P.n_ops


def host_inputs(inp, b):
    f = np.float32
    w_in = np.asarray(inp["w_in"], f)
    o = np.cumsum((0, 512, 512, 512, 4, 4, 512, 512, 512, 512, 4, 512, 1024, 1024))
    qa, ka, va, a_a, b_a, za, qf, kf, vf, f_f, zf, ga, gf = [w_in[:, o[i]:o[i + 1]] for i in range(13)]
    pairs = []
    for g in range(4):
        pairs += [ga[:, 2 * g * 128:(2 * g + 2) * 128], gf[:, 2 * g * 128:(2 * g + 2) * 128]]
    wB = np.ascontiguousarray(np.concatenate([qa, ka, va, za, qf, zf] + pairs, axis=1))
    c = np.asarray(inp["c"], f)[b]
    b_ada = np.asarray(inp["b_ada"], f)
    til = lambda v: np.ascontiguousarray(np.broadcast_to(np.tile(np.asarray(v, f), 4)[None, :], (128, 16)))
    col = lambda v: np.ascontiguousarray(np.asarray(v, f).reshape(128, 1))
    idx = np.arange(128)
    sel = np.zeros((16, 16, 128), f)
    for r in range(16):
        sel[r, r, :] = 1
    return {
        "x": np.ascontiguousarray(np.asarray(inp["x"], f)[b]),
        "cT": np.ascontiguousarray(c.reshape(8, 128).T),
        "w_ada": np.ascontiguousarray(np.asarray(inp["w_ada"], f)),
        "b_adaT": np.ascontiguousarray(b_ada[:2048].reshape(16, 128).T),
        "b_gate": np.ascontiguousarray(np.broadcast_to(b_ada[None, 2048:], (128, 1024))),
        "g_normT": np.ascontiguousarray(np.asarray(inp["g_norm"], f).reshape(8, 128).T),
        "wAk": np.ascontiguousarray(kf), "wAv": np.ascontiguousarray(vf), "wAf": np.ascontiguousarray(f_f),
        "wB": wB, "wab": np.ascontiguousarray(np.concatenate([a_a, b_a], axis=1)),
        "conv_wT": np.ascontiguousarray(np.asarray(inp["conv_w"], f).T.reshape(12, 128, 4).transpose(1, 0, 2)),
        "alog_bc": til(inp["A_log"]), "dtb_bc": til(inp["dt_bias"]), "bf_bc": til(inp["b_f"]),
        "g_gdn": col(inp["g_gdn_out"]), "g_q": col(inp["g_q_fox"]), "g_k": col(inp["g_k_fox"]),
        "g_q_row": np.ascontiguousarray(np.asarray(inp["g_q_fox"], f).reshape(1, 128)),
        "g_k_row": np.ascontiguousarray(np.asarray(inp["g_k_fox"], f).reshape(1, 128)),
        "w_o_gdn": np.ascontiguousarray(np.asarray(inp["w_o_gdn"], f)),
        "w_o_fox": np.ascontiguousarray(np.asarray(inp["w_o_fox"], f)),
        "w_out": np.ascontiguousarray(np.asarray(inp["w_out"], f)),
        "ident": np.eye(128, dtype=f),
        "tri": (idx[:, None] <= idx[None, :]).astype(f),
        "ml0": ((idx[None, :] < idx[:, None]) & (idx[None, :] // 32 == idx[:, None] // 32)).astype(f),
        "ml1": ((idx[None, :] < idx[:, None]) & (idx[None, :] // 32 != idx[:, None] // 32)
                & (idx[None, :] // 64 == idx[:, None] // 64)).astype(f),
        "ml2": ((idx[None, :] < idx[:, None]) & (idx[None, :] // 64 != idx[:, None] // 64)).astype(f),
        "sel4": np.ascontiguousarray(np.kron(np.eye(4, dtype=f), np.ones((1, 128), f))),
    }


def kernel(**inputs):
    x = np.asarray(inputs["x"])
    B, T, _ = x.shape
    nc, _ = build(T)
    in_maps = [host_inputs(inputs, b) for b in range(B)]
    res = run_bass_kernel_spmd(nc, in_maps, core_ids=list(range(B)))
    return np.stack([np.asarray(r["out"], np.float32) for r in res.results], axis=0)
```

```python
import numpy as np
from contextlib import ExitStack
import concourse.bass as bass
import concourse.mybir as mybir
from concourse.bass_utils import run_bass_kernel_spmd

F32 = mybir.dt.float32
BF16 = mybir.dt.bfloat16
AF = mybir.ActivationFunctionType
ALU = mybir.AluOpType
NDSEM = 6
CE_T = 6
D = 1024
EPS = 1e-6


class _Stop(Exception):
    pass


class Prog:
    def __init__(self, nc):
        self.nc = nc
        self.ops = []

    limit = None

    excl = ("POT", "PN", "G0", "G1", ("PM", 0), ("PM", 1), ("PO", 0), ("PO", 1))

    def add(self, eng, fn, r=(), w=(), dma=False):
        if self.limit is not None and len(self.ops) >= self.limit:
            raise _Stop()
        w = tuple(w) + tuple(k for k in r if k in self.excl and k not in w)
        self.ops.append((eng, fn, tuple(r), tuple(w), dma))

    def pe(self, fn, r=(), w=()):
        self.add("pe", fn, r, w)

    def dma(self, q, fn, r=(), w=()):
        self.add(q, fn, r, w, dma=True)

    def emit(self):
        nc = self.nc
        ops = self.ops
        n = len(ops)
        last_w = {}
        readers = {}
        deps = [None] * n
        for i, (eng, fn, r, w, dma) in enumerate(ops):
            d = set()
            for k in r:
                if k in last_w:
                    d.add(last_w[k])
            for k in w:
                if k in last_w:
                    d.add(last_w[k])
                d.update(readers.get(k, ()))
            d.discard(i)
            deps[i] = d
            for k in r:
                readers.setdefault(k, []).append(i)
            for k in w:
                last_w[k] = i
                readers[k] = []
        needs_sig = [False] * n
        for i in range(n):
            for j in deps[i]:
                if ops[j][4]:
                    continue
                if ops[j][0] == "pe" and ops[i][0] == "pe" and not ops[i][4]:
                    continue
                needs_sig[j] = True
        engs = sorted(set(o[0] for o in ops))
        cnt = {e: 0 for e in engs}
        sig = [None] * n
        dma_cnt = {e: 0 for e in engs}
        dma_prev = {}
        pre_wait = [None] * n
        for i, (eng, fn, r, w, dma) in enumerate(ops):
            if dma:
                slot = dma_cnt[eng] % NDSEM
                dma_cnt[eng] += 1
                prev = dma_prev.get((eng, slot), 0)
                if prev:
                    pre_wait[i] = (("d", eng, slot), prev)
                dma_prev[(eng, slot)] = prev + 16
                sig[i] = (("d", eng, slot), prev + 16)
            elif needs_sig[i]:
                cnt[eng] += 1
                sig[i] = (("c", eng), cnt[eng])
        semkeys = set(s_[0] for s_ in sig if s_ is not None)
        stack = ExitStack()
        sems = {}
        for k in sorted(semkeys, key=str):
            sems[k] = stack.enter_context(nc.semaphore("s_" + "_".join(map(str, k))))
        seen = {e: {} for e in engs}
        plan = {e: [] for e in engs}
        last_dma = {}
        for i, (eng, fn, r, w, dma) in enumerate(ops):
            waits = {}
            if pre_wait[i] is not None:
                k, v = pre_wait[i]
                waits[k] = max(waits.get(k, 0), v)
            for j in deps[i]:
                if sig[j] is None:
                    continue
                k, v = sig[j]
                if k == ("c", "pe") and eng == "pe" and not dma:
                    continue
                waits[k] = max(waits.get(k, 0), v)
            wl = []
            for k, v in waits.items():
                if seen[eng].get(k, 0) >= v:
                    continue
                seen[eng][k] = v
                wl.append((k, v))
            plan[eng].append((wl, fn, sig[i], dma))
            if dma:
                last_dma[sig[i][0]] = sig[i][1]
        self.n_ops = n
        block = stack.enter_context(nc.Block())

        def mk(eng):
            def body(e):
                for wl, fn, sg, dma in plan[eng]:
                    for k, v in wl:
                        e.wait_ge(sems[k], v)
                    ins = fn(e)
                    if sg is not None:
                        ins.then_inc(sems[sg[0]], 16 if dma else 1)
                for k, v in last_dma.items():
                    if k[1] == eng:
                        e.wait_ge(sems[k], v)
            return body

        reg = {"pe": block.tensor, "act": block.scalar, "dve": block.vector,
               "pool": block.gpsimd, "sp": block.sync}
        for eng in engs:
            reg[eng](mk(eng))
        stack.close()


def build(T, stage=9, dq='pool', PC='pool', limit=None):
    NT = T // 512
    NBLK = T // 128
    nc = bass.Bass("TRN2", target_bir_lowering=False)
    di = lambda name, shape, dt=F32: nc.dram_tensor(name, list(shape), dt, kind="ExternalInput").ap()
    x_d = di("x", [T, D])
    cT_d = di("cT", [128, 8])
    wada_d = di("w_ada", [D, 3 * D])
    badaT_d = di("b_adaT", [128, 16])
    bgate_d = di("b_gate", [128, D])
    gnT_d = di("g_normT", [128, 8])
    wAk_d = di("wAk", [D, 512])
    wAv_d = di("wAv", [D, 512])
    wAf_d = di("wAf", [D, 4])
    wB_d = di("wB", [D, 5120])
    wab_d = di("wab", [D, 8])
    convT_d = di("conv_wT", [128, 12, 4])
    alog_d = di("alog_bc", [128, 16])
    dtb_d = di("dtb_bc", [128, 16])
    bf_d = di("bf_bc", [128, 16])
    ggdn_d = di("g_gdn", [128, 1])
    gq_d = di("g_q", [128, 1])
    gk_d = di("g_k", [128, 1])
    gqr_d = di("g_q_row", [1, 128])
    gkr_d = di("g_k_row", [1, 128])
    wog_d = di("w_o_gdn", [512, D])
    wof_d = di("w_o_fox", [512, D])
    wout_d = di("w_out", [D, D])
    ident_d = di("ident", [128, 128])
    tri_d = di("tri", [128, 128])
    ml0_d = di("ml0", [128, 128])
    ml1_d = di("ml1", [128, 128])
    ml2_d = di("ml2", [128, 128])
    sel_d = di("sel4", [4, 512])
    out_d = nc.dram_tensor("out", [T, D], F32, kind="ExternalOutput").ap()
    wBbf_d = nc.dram_tensor("wB_bf", [D, 5120], BF16).ap()
    KT_d = nc.dram_tensor("KT_s", [4, 128, T], BF16).ap()
    V1_d = nc.dram_tensor("V1_s", [NBLK, 128, 516], BF16).ap()

    st = ExitStack()
    sb = lambda name, shape, dt=F32: st.enter_context(nc.sbuf_tensor(name, list(shape), dt))
    ps = lambda name, shape, dt=F32: st.enter_context(nc.psum_tensor(name, list(shape), dt))
    P = Prog(nc)
    P.limit = limit

    def dma(q, out, in_, r, w):
        if q == "pool":
            q = dq
        P.dma(q, lambda e: e.dma_start(out=out, in_=in_), r, w)

    def mm(out, lhsT, rhs, start, stop, r, w):
        P.pe(lambda e: e.matmul(out, lhsT=lhsT, rhs=rhs, start=start, stop=stop), r, w)

    def tr(out, in_, idn, r, w):
        P.pe(lambda e: e.transpose(out=out, in_=in_, identity=idn), r, w)

    def act(out, in_, func, r, w, scale=1.0, bias=None, accum=None, eng="act"):
        kw = {}
        if bias is not None:
            kw["bias"] = bias
        if accum is not None:
            kw["accum_out"] = accum
        P.add(eng, lambda e: e.activation(out=out, in_=in_, func=func, scale=scale, **kw), r, w)

    def ts(eng, out, in0, s1, s2, op0, op1, r, w):
        if s2 is None:
            P.add(eng, lambda e: e.tensor_scalar(out=out, in0=in0, scalar1=s1, scalar2=None, op0=op0), r, w)
        else:
            P.add(eng, lambda e: e.tensor_scalar(out=out, in0=in0, scalar1=s1, scalar2=s2, op0=op0, op1=op1), r, w)

    def tt(eng, out, in0, in1, op, r, w):
        P.add(eng, lambda e: e.tensor_tensor(out=out, in0=in0, in1=in1, op=op), r, w)

    def stt(eng, out, in0, scalar, in1, op0, op1, r, w):
        P.add(eng, lambda e: e.scalar_tensor_tensor(out=out, in0=in0, scalar=scalar, in1=in1, op0=op0, op1=op1), r, w)

    def cp(eng, out, in_, r, w):
        if eng == "act":
            P.add(eng, lambda e: e.activation(out=out, in_=in_, func=AF.Copy), r, w)
        else:
            P.add(eng, lambda e: e.tensor_copy(out=out, in_=in_), r, w)

    def mset(eng, ap, val, w):
        P.add(eng, lambda e: e.memset(ap, val), (), w)

    def recip(out, in_, r, w):
        P.add("dve", lambda e: e.reciprocal(out=out, in_=in_), r, w)

    identf = sb("identf", [128, 128]); trif = sb("trif", [128, 128]); ml0f = sb("ml0f", [128, 128]); ml1f = sb("ml1f", [128, 128]); ml2f = sb("ml2f", [128, 128])
    onesf = sb("onesf", [128, 128]); identb = sb("identb", [128, 128], BF16); onesb = sb("onesb", [128, 128], BF16)
    sel4 = sb("sel4s", [4, 512])
    onec = sb("onec", [128, 1]); epsc = sb("epsc", [128, 1])
    cT = sb("cTs", [128, 8]); badaT = sb("badaTs", [128, 16]); gnT = sb("gnTs", [128, 8])
    aT = sb("aT", [128, 8]); shT = sb("shT", [128, 8]); modT = sb("modT", [128, 16])
    convw = sb("convw", [128, 12, 4]); negA = sb("negA", [128, 16]); dtb = sb("dtb", [128, 16]); bfb = sb("bfb", [128, 16])
    ggdn = sb("ggdn", [128, 1]); gq = sb("gqs", [128, 1]); gk = sb("gks", [128, 1]); gqs = sb("gqsc", [128, 1])
    grow = sb("grow", [1, 256]); grow2 = sb("grow2", [1, 256]); gmx = sb("gmx", [1, 2]); mb = sb("mb", [128, 1]); negm = sb("negm", [128, 1])
    woutb = sb("woutb", [128, 8, D], BF16); wogb = sb("wogb", [128, 4, D], BF16); wofb = sb("wofb", [128, 4, D], BF16)
    wabb = sb("wabb", [128, 8, 8], BF16); wAfb = sb("wAfb", [128, 8, 4], BF16)
    wg = [sb("wg%d" % i, [128, 8, 512], BF16) for i in range(2)]
    stg = [sb("stg%d" % i, [128, 1024]) for i in range(2)]
    stgb = [sb("stgb%d" % i, [128, 512], BF16) for i in range(2)]
    hnb = [sb("hn%d" % i, [128, D], BF16) for i in range(2)]
    ssq = sb("ssq", [128, 4]); rstd = sb("rstd", [128, 4])
    hT = sb("hT", [128, 8, 512], BF16)
    sq = [sb("sq%d" % i, [128, 512], BF16) for i in range(2)]; rst = [sb("rst%d" % i, [128, 512]) for i in range(2)]
    ktile = [sb("ktile%d" % i, [128, 512], BF16) for i in range(2)]
    v1blk = [sb("v1blk%d" % i, [128, 4, 129], BF16) for i in range(2)]
    LOGF = sb("LOGF", [128, NBLK * 4]); CTOK = sb("CTOK", [128, NBLK * 4]); BASE = sb("BASE", [128, NBLK * 4])
    TOTS = sb("TOTS", [128, NBLK * 4]); REFM = sb("REFM", [128, NBLK * 4])
    tmp16 = sb("tmp16", [128, 16]); tmp16b = sb("tmp16b", [128, 16])
    U = sb("U", [128, 4, 515]); halo = sb("halo", [128, 12, 3])
    cBv = lambda k: U[:, k // 4, (k % 4) * 128:(k % 4 + 1) * 128]
    cBk = lambda k: ("U", k // 4)
    gatev = lambda n2: U[:, 2 + n2, 0:512]
    gatek = lambda n2: ("U", 2 + n2)
    cacc = [sb("cacc%d" % i, [128, 512]) for i in range(2)]
    etmp = [sb("etmp%d" % i, [128, 512]) for i in range(2)]
    QKV = sb("QKV", [128, 12, 512], BF16)
    SZ = sb("SZ", [128, 8, 512], BF16); QT = sb("QT", [128, 4, 512], BF16)
    SG = sb("SG", [128, 4, 512], BF16)
    oaT = sb("oaT", [128, 4, 512], BF16); ofT = sb("ofT", [128, 4, 512], BF16)
    mT = sb("mT", [128, 8, 512], BF16); ma = sb("ma", [128, 512])
    gtok = sb("gtok", [128, 16]); btok = sb("btok", [128, 16]); nbeta = sb("nbeta", [128, 16])
    gc = sb("gc", [128, 16]); gsum = sb("gsum", [128, 16]); egc = sb("egc", [128, 16]); egl = sb("egl", [128, 16])
    ekd = sb("ekd", [128, 16]); bek = sb("bek", [128, 16]); gcrow4 = sb("gcrow4", [4, 512])
    H4 = range(4)
    mkf = lambda nm: [sb("%s_%d" % (nm, h), [128, 128]) for h in H4]
    mkb = lambda nm: [sb("%s_%d" % (nm, h), [128, 128], BF16) for h in H4]
    t1 = mkf("t1"); t2 = mkf("t2"); x1 = mkf("x1"); pt1 = mkf("pt1"); Uf = mkf("Uf"); Of = mkf("Of")
    Xb = [[sb("Xb%d_%d" % (i, h), [128, 128], BF16) for i in range(2)] for h in H4]
    Yb = [[sb("Yb%d_%d" % (i, h), [128, 128], BF16) for i in range(2)] for h in H4]
    Rb = mkb("Rb"); Tb = mkb("Tb"); Wb = mkb("Wb"); X1b = mkb("X1b"); X2b = mkb("X2b"); PTm = mkb("PTm")
    Vb = mkb("Vb"); Kb = mkb("Kb"); Kd = mkb("Kd"); WT = mkb("WT"); vnew = mkb("vnew"); onb = mkb("onb")
    ms = [sb("ms_%d" % h, [128, 1]) for h in H4]; junk128 = sb("junk128", [128, 128], BF16)
    Sf = [sb("Sf%d" % h, [128, 128]) for h in range(4)]
    Sb = [sb("Sb%d" % h, [128, 128], BF16) for h in range(4)]
    crt = sb("crt", [128, 4]); oden = sb("oden", [128, 4]); Zz = sb("Zz", [128, 4, 4]); crx = sb("crx", [36, 512], BF16); crh = sb("crh", [4, 512])
    ones36 = sb("ones36", [36, 128], BF16)
    BT = sb("BT", [128, NBLK])
    kst = [sb("kst%d" % i, [128, 512], BF16) for i in range(2)]
    vst = [sb("vst%d" % i, [128, 4, 516], BF16) for i in range(2)]
    ptt = [sb("ptt%d" % i, [128, 512], BF16) for i in range(2)]
    trib = sb("trib", [128, 128], BF16)
    rl = sb("rl", [128, 1]); onf = sb("onf", [128, 128], BF16)

    POT = ps("POT", [128, 512])
    PM = [ps("PM%d" % i, [128, 512]) for i in range(2)]
    PN = ps("PN", [128, 512])
    G0 = ps("G0", [128, 4, 128]); G1 = ps("G1", [128, 4, 128])
    PO = [ps("PO%d" % i, [128, 4, 128]) for i in range(2)]

    PM = [PM[0][:], PM[1][:]] + [X_[:].rearrange("p a b -> p (a b)") for X_ in (G0, G1, PO[0], PO[1])]
    PMK = [("PM", 0), ("PM", 1), "G0", "G1", ("PO", 0), ("PO", 1)]
    POa = [X_[:].rearrange("p a b -> p (a b)")[:, 0:129] for X_ in (PO[0], PO[1], G0, G1)]
    GB = [G0, G1, PO[0], PO[1]]
    GBk = ["G0", "G1", ("PO", 0), ("PO", 1)]
    POk = [("PO", 0), ("PO", 1), "G0", "G1"]
    for name, dst, src in (("identf", identf, ident_d), ("trif", trif, tri_d), ("ml0f", ml0f, ml0_d), ("ml1f", ml1f, ml1_d), ("ml2f", ml2f, ml2_d),
                           ("sel4", sel4, sel_d), ("cT", cT, cT_d), ("badaT", badaT, badaT_d), ("gnT", gnT, gnT_d),
                           ("convw", convw, convT_d), ("negA", negA, alog_d), ("dtb", dtb, dtb_d), ("bfb", bfb, bf_d),
                           ("ggdn", ggdn, ggdn_d), ("gq", gq, gq_d), ("gk", gk, gk_d)):
        dma("sp", dst[:], src, [], [name])
    for n2 in range(2):
        dma("sp", gatev(n2), bgate_d[:, n2 * 512:(n2 + 1) * 512], [], [gatek(n2)])
    dma("sp", grow[:, 0:128], gqr_d, [], ["grow"])
    dma("sp", grow[:, 128:256], gkr_d, [], ["grow"])
    mset("dve", onesf[:], 1.0, ["onesf"]); mset("dve", onesb[:], 1.0, ["onesb"]); mset("dve", onec[:], 1.0, ["onec"])
    mset("dve", epsc[:], EPS, ["epsc"]); mset("dve", ones36[:], 1.0, ["ones36"]); mset("dve", crx[:], 0.0, ["crx"])
    mset("dve", Zz[:], 0.0, ["Zz"]); mset("dve", halo[:], 0.0, ["halo"])
    for i in range(2):
        mset("dve", v1blk[i][:], 1.0, [("v1blk", i)])
    for h in range(4):
        mset("dve", Sf[h][:], 0.0, [("Sf", h)]); mset("dve", Sb[h][:], 0.0, [("Sb", h)])
    cp("dve", identb[:], identf[:], ["identf"], ["identb"])
    cp("dve", trib[:], trif[:], ["trif"], ["trib"])
    ts("dve", convw[:], convw[:], 0.5, None, ALU.mult, None, ["convw"], ["convw"])
    ts("dve", ggdn[:], ggdn[:], 0.5, None, ALU.mult, None, ["ggdn"], ["ggdn"])
    act(negA[:], negA[:], AF.Exp, ["negA"], ["negA"])
    ts("dve", negA[:], negA[:], -1.0, None, ALU.mult, None, ["negA"], ["negA"])
    ts("dve", gqs[:], gq[:], 128.0 ** -0.5, None, ALU.mult, None, ["gq"], ["gqs"])
    ts("dve", grow2[:], grow[:], -1.0, None, ALU.mult, None, ["grow"], ["grow2"])
    tt("dve", grow[:], grow[:], grow2[:], ALU.max, ["grow", "grow2"], ["grow"])
    P.add("dve", lambda e: e.reduce_max(out=gmx[:, 0:1], in_=grow[:, 0:128], axis=mybir.AxisListType.X), ["grow"], ["gmx"])
    P.add("dve", lambda e: e.reduce_max(out=gmx[:, 1:2], in_=grow[:, 128:256], axis=mybir.AxisListType.X), ["grow"], ["gmx"])
    ts("dve", gmx[:, 0:1], gmx[:, 0:1], gmx[:, 1:2], -(128.0 ** 0.5), ALU.mult, ALU.mult, ["gmx"], ["gmx"])
    mm(PN[:, 0:1], onesf[0:1, :], gmx[:, 0:1], True, True, ["onesf", "gmx"], ["PN"])
    cp("dve", negm[:], PN[:, 0:1], ["PN"], ["negm"])

    for j in range(16):
        s_ = stg[j % 2]
        dma("sp", s_[:].rearrange("p (k n) -> p k n", k=8), wada_d[:, j * 128:(j + 1) * 128].rearrange("(k p) n -> p k n", p=128),
            [], [("stg", j % 2)])
        for k in range(8):
            mm(PN[:, 16 + j:17 + j], s_[:, k * 128:(k + 1) * 128], cT[:, k:k + 1], k == 0, k == 7,
               [("stg", j % 2), "cT"], ["PN"])
    for j in range(16):
        tt("dve", modT[:, j:j + 1], PN[:, 16 + j:17 + j], badaT[:, j:j + 1], ALU.add, ["PN", "badaT"], ["modT"])
    cp("dve", shT[:], modT[:, 0:8], ["modT"], ["shT"])
    stt("dve", aT[:], modT[:, 8:16], 1.0, gnT[:], ALU.add, ALU.mult, ["modT", "gnT"], ["aT"])
    for k in range(8):
        ts("dve", cBv(k), onesf[:], cT[:, k:k + 1], None, ALU.mult, None, ["onesf", "cT"], [cBk(k)])
    for k in range(8):
        s_ = stg[k % 2]
        dma("sp", s_[:], wada_d[k * 128:(k + 1) * 128, 2048:3072], [], [("stg", k % 2)])
        for n2 in range(2):
            mm(PM[n2][:], cBv(k), s_[:, n2 * 512:(n2 + 1) * 512], k == 0, k == 7, [cBk(k), ("stg", k % 2)], [("PM", n2)])
    for n2 in range(2):
        tt("dve", gatev(n2), PM[n2][:], gatev(n2), ALU.add, [("PM", n2), gatek(n2)], [gatek(n2)])
    si = 0
    for k in range(8):
        s_ = stg[si % 2]
        dma("sp", s_[:], wout_d[k * 128:(k + 1) * 128, :], [], [("stg", si % 2)])
        for n2 in range(2):
            stt("dve", woutb[:, k, n2 * 512:(n2 + 1) * 512], s_[:, n2 * 512:(n2 + 1) * 512], 0.5, gatev(n2), ALU.mult, ALU.mult,
                [("stg", si % 2), gatek(n2)], ["woutb"])
        si += 1
    for (wd, wb_, nm) in ((wog_d, wogb, "wogb"), (wof_d, wofb, "wofb")):
        for k in range(4):
            s_ = stg[si % 2]
            dma("sp", s_[:], wd[k * 128:(k + 1) * 128, :], [], [("stg", si % 2)])
            act(wb_[:, k, :], s_[:], AF.Identity, [("stg", si % 2)], [nm], scale=(0.5 if nm == "wofb" else 1.0))
            si += 1
    s_ = stg[si % 2]
    dma("sp", s_[:, 0:64].rearrange("p (k n) -> p k n", k=8), wab_d.rearrange("(k p) n -> p k n", p=128), [], [("stg", si % 2)])
    cp("dve", wabb[:], s_[:, 0:64].rearrange("p (k n) -> p k n", k=8), [("stg", si % 2)], ["wabb"])
    si += 1
    s_ = stg[si % 2]
    dma("sp", s_[:, 0:32].rearrange("p (k n) -> p k n", k=8), wAf_d.rearrange("(k p) n -> p k n", p=128), [], [("stg", si % 2)])
    cp("dve", wAfb[:], s_[:, 0:32].rearrange("p (k n) -> p k n", k=8), [("stg", si % 2)], ["wAfb"])
    si += 1
    for k in range(8):
        for g in range(5):
            s_ = stg[si % 2]
            dma("sp", s_[:], wB_d[k * 128:(k + 1) * 128, g * 1024:(g + 1) * 1024], [], [("stg", si % 2)])
            for hh in range(2):
                cp("act" if hh else "dve", stgb[hh][:], s_[:, hh * 512:(hh + 1) * 512], [("stg", si % 2)], [("stgb", hh)])
                dma("pool", wBbf_d[k * 128:(k + 1) * 128, g * 1024 + hh * 512:g * 1024 + (hh + 1) * 512], stgb[hh][:],
                    [("stgb", hh)], ["wBbf"])
            si += 1
    for (wd, wi) in ((wAk_d, 0), (wAv_d, 1)):
        for k in range(8):
            s_ = stg[si % 2]
            dma("sp", s_[:, 0:512], wd[k * 128:(k + 1) * 128, :], [], [("stg", si % 2)])
            cp("act", wg[wi][:, k, :], s_[:, 0:512], [("stg", si % 2)], [("wg", wi)])
            si += 1

    if stage == 0:
        P.emit(); st.close(); return nc, P.n_ops
    xi = [0]

    def make_hT(tt_):
        t0 = tt_ * 512
        for b in range(4):
            s_i = xi[0] % 2
            xi[0] += 1
            xb = stg[s_i]
            hb = hnb[b % 2]
            kh = ("hn", b % 2)
            dma("sp", xb[:], x_d[t0 + b * 128:t0 + (b + 1) * 128, :], [], [("stg", s_i)])
            mset("dve", ssq[:, b:b + 1], 0.0, [("ssq", b)])
            act(hb[:], xb[:], AF.Square, [("stg", s_i), ("ssq", b)], [kh, ("ssq", b)], scale=1.0 / 32, accum=ssq[:, b:b + 1])
            act(rstd[:, b:b + 1], ssq[:, b:b + 1], AF.Ln, [("ssq", b), "epsc"], [("rstd", b)], bias=epsc[:])
            act(rstd[:, b:b + 1], rstd[:, b:b + 1], AF.Exp, [("rstd", b)], [("rstd", b)], scale=-0.5)
            ts("dve", hb[:], xb[:], rstd[:, b:b + 1], None, ALU.mult, None, [("stg", s_i), ("rstd", b)], [kh])
            for k4 in range(2):
                i = next_pm()
                for k in range(k4 * 4, k4 * 4 + 4):
                    mm(PM[i][:, (k % 4) * 128:(k % 4 + 1) * 128], hb[:, k * 128:(k + 1) * 128], identb[:], True, True,
                       [kh, "identb"], [PMK[i]])
                for k in range(k4 * 4, k4 * 4 + 4):
                    src = PM[i][:, (k % 4) * 128:(k % 4 + 1) * 128]
                    if k % 2 == 0:
                        act(hT[:, k, b * 128:(b + 1) * 128], src, AF.Identity, [PMK[i], "aT", "shT"], [("hT", k)],
                            scale=aT[:, k:k + 1], bias=shT[:, k:k + 1])
                    else:
                        ts("dve", hT[:, k, b * 128:(b + 1) * 128], src, aT[:, k:k + 1], shT[:, k:k + 1], ALU.mult, ALU.add,
                           [PMK[i], "aT", "shT"], [("hT", k)])

    pmi = [0]
    nrm_n = [0]

    def norm_gen(src_fn, out_ap, out_keys, mean_scale, scalar, scalar_keys, post=None):
        src_ap, src_keys = src_fn()
        j = nrm_n[0] % 2
        nrm_n[0] += 1
        act(sq[j][:], src_ap, AF.Square, src_keys, [("sq", j)])
        yield
        i = next_pm()
        mm(PM[i][:], onesb[:], sq[j][:], True, True, ["onesb", ("sq", j)], [PMK[i]])
        yield
        act(rst[j][:], PM[i][:], AF.Ln, [PMK[i], "epsc"], [("rst", j)], scale=mean_scale, bias=epsc[:])
        yield
        act(rst[j][:], rst[j][:], AF.Exp, [("rst", j)], [("rst", j)], scale=-0.5)
        yield
        stt("dve", out_ap, src_ap, scalar, rst[j][:], ALU.mult, ALU.mult, src_keys + scalar_keys + [("rst", j)], out_keys)
        if post is not None:
            post()

    def run_staggered(pend, maxlive=2):
        gens = []
        while gens or pend:
            if pend and len(gens) < maxlive:
                gens.append(pend.pop(0))
            for g_ in list(gens):
                try:
                    next(g_)
                except StopIteration:
                    gens.remove(g_)

    def next_pm():
        i = pmi[0] % 6
        pmi[0] += 1
        return i

    for tt_ in range(NT):
        t0 = tt_ * 512
        make_hT(tt_)
        def k_src(h):
            def f():
                i = next_pm()
                for k in range(8):
                    mm(PM[i][:], wg[0][:, k, h * 128:(h + 1) * 128], hT[:, k, :], k == 0, k == 7, [("wg", 0), ("hT", k)], [PMK[i]])
                return PM[i][:], [PMK[i]]
            return f

        def k_post(h):
            return lambda: dma("pool", KT_d[h, :, t0:t0 + 512], ktile[h % 2][:], [("ktile", h % 2)], ["KT_d"])

        run_staggered([norm_gen(k_src(h), ktile[h % 2][:], [("ktile", h % 2)], 1.0 / 128, gk[:], ["gk"], post=k_post(h))
                       for h in range(4)])
        for b in range(4):
            i = next_pm()
            vi = b % 2
            for k in range(8):
                mm(PM[i][:], hT[:, k, b * 128:(b + 1) * 128], wg[1][:, k, :], k == 0, k == 7, [("wg", 1), ("hT", k)], [PMK[i]])
            cp("act", v1blk[vi][:, :, 0:128], PM[i][:].rearrange("p (h d) -> p h d", h=4), [PMK[i]], [("v1blk", vi)])
            dma("pool", V1_d[tt_ * 4 + b], v1blk[vi][:].rearrange("p h d -> p (h d)"), [("v1blk", vi)], ["V1_d"])
            for k in range(8):
                mm(PN[:, b * 4:(b + 1) * 4], hT[:, k, b * 128:(b + 1) * 128], wAfb[:, k, :], k == 0, k == 7,
                   ["wAfb", ("hT", k)], ["PN"])
        tt("dve", tmp16[:], PN[:, 0:16], bfb[:], ALU.add, ["PN", "bfb"], ["tmp16"])
        act(tmp16[:], tmp16[:], AF.Exp, ["tmp16"], ["tmp16"], scale=-1.0)
        act(tmp16[:], tmp16[:], AF.Ln, ["tmp16", "onec"], ["tmp16"], bias=onec[:])
        ts("dve", LOGF[:, tt_ * 16:(tt_ + 1) * 16], tmp16[:], -1.0, None, ALU.mult, None, ["tmp16"], ["LOGF"])

    NB4 = NBLK * 4
    for c0 in range(0, NB4, 512):
        c1 = min(NB4, c0 + 512)
        mm(PM[0][:, 0:c1 - c0], trif[:], LOGF[:, c0:c1], True, True, ["trif", "LOGF"], [("PM", 0)])
        mm(PM[1][:, 0:c1 - c0], onesf[:], LOGF[:, c0:c1], True, True, ["onesf", "LOGF"], [("PM", 1)])
        cp("dve", CTOK[:, c0:c1], PM[0][:, 0:c1 - c0], [("PM", 0)], ["CTOK"])
        cp("dve", TOTS[:, c0:c1], PM[1][:, 0:c1 - c0], [("PM", 1)], ["TOTS"])
    mset("dve", BASE[:, 0:4], 0.0, ["BASE"])
    for bl in range(1, NBLK):
        tt("dve", BASE[:, bl * 4:(bl + 1) * 4], BASE[:, (bl - 1) * 4:bl * 4], TOTS[:, (bl - 1) * 4:bl * 4], ALU.add,
           ["BASE", "TOTS"], ["BASE"])
    tt("dve", CTOK[:], CTOK[:], BASE[:], ALU.add, ["CTOK", "BASE"], ["CTOK"])
    ts("dve", REFM[:], BASE[:], negm[:], None, ALU.add, None, ["BASE", "negm"], ["REFM"])

    if stage == 1:
        P.emit(); st.close(); return nc, P.n_ops
    wgi = [0]
    kvi = [0]
    G0k = lambda s: "G0"
    G1k = lambda s: "G1"

    WORDER = [0, 1, 2, 3, 5, 4, 6, 7, 8, 9]
    wpending = {}

    def _issue_w(seq):
        i = seq % 2
        g = WORDER[seq % 10]
        dma("sp", wg[i][:], wBbf_d[:, g * 512:(g + 1) * 512].rearrange("(k p) n -> p k n", p=128),
            ["wBbf"], [("wg", i)])
        wpending[seq] = i

    def load_wgroup(g):
        seq = wgi[0]
        wgi[0] += 1
        assert WORDER[seq % 10] == g
        if seq not in wpending:
            _issue_w(seq)
        i = wpending.pop(seq)
        if seq + 1 < 10 * NT:
            wnext[0] = seq + 1
        return i

    wnext = [None]

    def prefetch_w():
        if wnext[0] is not None and wnext[0] not in wpending:
            _issue_w(wnext[0])
        wnext[0] = None

    def proj_chunk(wi, c):
        i = next_pm()
        for k in range(8):
            mm(PM[i][:], wg[wi][:, k, c * 128:(c + 1) * 128], hT[:, k, :], k == 0, k == 7, [("wg", wi), ("hT", k)], [PMK[i]])
        if c == 0:
            prefetch_w()
        return i

    def silu_from(src_ap, src_keys, out_ap, out_keys, ei):
        e_ = etmp[ei]
        act(e_[:], src_ap, AF.Exp, src_keys, [("etmp", ei)], scale=-1.0)
        ts("dve", e_[:], e_[:], 1.0, None, ALU.add, None, [("etmp", ei)], [("etmp", ei)])
        recip(e_[:], e_[:], [("etmp", ei)], [("etmp", ei)])
        tt("dve", out_ap, src_ap, e_[:], ALU.mult, src_keys + [("etmp", ei)], out_keys)

    def passB_tile(tt_):
        t0 = tt_ * 512
        make_hT(tt_)
        for b in range(4):
            for k in range(8):
                mm(PN[:, 32 + b * 8:40 + b * 8], hT[:, k, b * 128:(b + 1) * 128], wabb[:, k, :], k == 0, k == 7,
                   ["wabb", ("hT", k)], ["PN"])
        if stage == 2:
            raise _Stop()
        def conv_chunk(g, c, wi):
            ch = g * 4 + c
            i = proj_chunk(wi, c)
            cp("dve", U[:, c, 0:3], halo[:, ch, :], ["halo"], [("U", c)])
            cp("act", U[:, c, 3:515], PM[i][:], [PMK[i]], [("U", c)])
            yield
            ci = ch % 2
            acc = cacc[ci]
            ts("dve", acc[:], U[:, c, 0:512], convw[:, ch, 0:1], None, ALU.mult, None, [("U", c), "convw"], [("cacc", ci)])
            for tap in range(1, 4):
                stt("dve", acc[:], U[:, c, tap:tap + 512], convw[:, ch, tap:tap + 1], acc[:], ALU.mult, ALU.add,
                    [("U", c), "convw", ("cacc", ci)], [("cacc", ci)])
            cp(PC, halo[:, ch, :], U[:, c, 512:515], [("U", c)], ["halo"])
            act(etmp[ci][:], acc[:], AF.Tanh, [("cacc", ci)], [("etmp", ci)])
            yield
            yield
            stt("dve", QKV[:, ch, :], etmp[ci][:], 1.0, acc[:], ALU.add, ALU.mult, [("etmp", ci), ("cacc", ci)], [("QKV", ch)])

        for g in range(3):
            wi = load_wgroup(g)
            gens = []
            pend = [conv_chunk(g, c, wi) for c in range(4)]
            while gens or pend:
                if pend:
                    gens.append(pend.pop(0))
                for g_ in list(gens):
                    try:
                        next(g_)
                    except StopIteration:
                        gens.remove(g_)
        if stage == 3:
            raise _Stop()
        for g in (3, 5):
            wi = load_wgroup(g)
            for c in range(4):
                i = proj_chunk(wi, c)
                zc = (0 if g == 3 else 4) + c
                ei = c % 2
                act(etmp[ei][:], PM[i][:], AF.Tanh, [PMK[i]], [("etmp", ei)], scale=0.5)
                stt("dve", SZ[:, zc, :], etmp[ei][:], 1.0, PM[i][:], ALU.add, ALU.mult, [("etmp", ei), PMK[i]], [("SZ", zc)])
        pab = PN[:, 32:64].rearrange("p (b c) -> p b c", b=4)
        tt("dve", tmp16[:].rearrange("p (b h) -> p b h", b=4), pab[:, :, 0:4], dtb[:].rearrange("p (b h) -> p b h", b=4), ALU.add,
           ["PN", "dtb"], ["tmp16"])
        act(tmp16[:], tmp16[:], AF.Exp, ["tmp16"], ["tmp16"])
        act(tmp16[:], tmp16[:], AF.Ln, ["tmp16", "onec"], ["tmp16"], bias=onec[:])
        tt("dve", gtok[:], tmp16[:], negA[:], ALU.mult, ["tmp16", "negA"], ["gtok"])
        act(tmp16b[:].rearrange("p (b h) -> p b h", b=4), pab[:, :, 4:8], AF.Exp, ["PN"], ["tmp16b"], scale=-1.0)
        ts("dve", tmp16b[:], tmp16b[:], 1.0, None, ALU.add, None, ["tmp16b"], ["tmp16b"])
        recip(btok[:], tmp16b[:], ["tmp16b"], ["btok"])
        ts("dve", nbeta[:], btok[:], -1.0, None, ALU.mult, None, ["btok"], ["nbeta"])
        mm(PN[:, 64:80], trif[:], gtok[:], True, True, ["trif", "gtok"], ["PN"])
        mm(PN[:, 80:96], onesf[:], gtok[:], True, True, ["onesf", "gtok"], ["PN"])
        cp("dve", gc[:], PN[:, 64:80], ["PN"], ["gc"])
        cp("dve", gsum[:], PN[:, 80:96], ["PN"], ["gsum"])
        for b in range(4):
            mm(PM[0][0:4, b * 128:(b + 1) * 128], gtok[:, b * 4:(b + 1) * 4], trif[:], True, True, ["trif", "gtok"], [("PM", 0)])
        cp("dve", gcrow4[:], PM[0][0:4, :], [("PM", 0)], ["gcrow4"])
        act(egc[:], gc[:], AF.Exp, ["gc"], ["egc"])
        act(egl[:], gsum[:], AF.Exp, ["gsum"], ["egl"])
        tt("dve", ekd[:], gsum[:], gc[:], ALU.subtract, ["gsum", "gc"], ["ekd"])
        act(ekd[:], ekd[:], AF.Exp, ["ekd"], ["ekd"])
        tt("dve", bek[:], btok[:], egc[:], ALU.mult, ["btok", "egc"], ["bek"])

        wi4 = load_wgroup(4)

        def qk_src(ch):
            return lambda: (QKV[:, ch, :], [("QKV", ch)])

        def qf_src(c):
            def f():
                i = proj_chunk(wi4, c)
                return PM[i][:], [PMK[i]]
            return f

        run_staggered([norm_gen(qk_src(ch), QKV[:, ch, :], [("QKV", ch)], 1.0, (128.0 ** -0.5) if ch < 4 else 1.0, [])
                       for ch in range(8)]
                      + [norm_gen(qf_src(c), QT[:, c, :], [("QT", c)], 1.0 / 128, gqs[:], ["gqs"]) for c in range(4)])

        if stage == 4:
            raise _Stop()
        ce_ = "dve" if tt_ >= CE_T else "act"

        def gdn_head(h):
            GBh = GB[h]
            gk = GBk[h]
            kK, kQ, kV = ("QKV", 4 + h), ("QKV", h), ("QKV", 8 + h)
            K_ = lambda nm: (nm, h)
            for b in range(4):
                bc = slice(b * 128, (b + 1) * 128)
                r_ = b * 4 + h
                Ka = QKV[:, 4 + h, bc]
                Qa = QKV[:, h, bc]
                Va = QKV[:, 8 + h, bc]
                mm(GBh[:, 0, :], sel4[:, h * 128:(h + 1) * 128], gcrow4[:, bc], True, True, ["sel4", "gcrow4"], [gk])
                yield
                ts("dve", t1[h][:], GBh[:, 0, :], gc[:, r_:r_ + 1], 0.0, ALU.subtract, ALU.min, [gk, "gc"], [K_("t1")])
                ts("dve", t2[h][:], GBh[:, 0, :], gc[:, r_:r_ + 1], 0.0, ALU.subtract, ALU.max, [gk, "gc"], [K_("t2")])
                yield
                act(t1[h][:], t1[h][:], AF.Exp, [K_("t1")], [K_("t1")])
                act(t2[h][:], t2[h][:], AF.Exp, [K_("t2")], [K_("t2")], scale=-1.0)
                mm(GBh[:, 1, :], Ka, Ka, True, True, [kK], [gk])
                mm(GBh[:, 2, :], Ka, Qa, True, True, [kK, kQ], [gk])
                yield
                stt("dve", x1[h][:], GBh[:, 1, :], nbeta[:, r_:r_ + 1], t2[h][:], ALU.mult, ALU.mult, [gk, "nbeta", K_("t2")], [K_("x1")])
                tt("dve", pt1[h][:], GBh[:, 2, :], t1[h][:], ALU.mult, [gk, K_("t1")], [K_("pt1")])
                yield
                tt(PC, Xb[h][0][:], x1[h][:], ml0f[:], ALU.mult, [K_("x1"), "ml0f"], [("Xb", 0, h)])
                tt(PC, X1b[h][:], x1[h][:], ml1f[:], ALU.mult, [K_("x1"), "ml1f"], [K_("X1b")])
                tt(PC, X2b[h][:], x1[h][:], ml2f[:], ALU.mult, [K_("x1"), "ml2f"], [K_("X2b")])
                tt(PC, PTm[h][:], pt1[h][:], trif[:], ALU.mult, [K_("pt1"), "trif"], [K_("PTm")])
                yield
                mm(GBh[:, 3, :], Xb[h][0][:], identb[:], True, True, [("Xb", 0, h), "identb"], [gk])
                yield
                cp("act", Yb[h][0][:], GBh[:, 3, :], [gk], [("Yb", 0, h)])
                tt("dve", Rb[h][:], GBh[:, 3, :], identb[:], ALU.add, [gk, "identb"], [K_("Rb")])
                yield
                cur = 0
                for lv in range(1, 5):
                    nx = 1 - cur
                    if lv <= 3:
                        mm(GBh[:, 0, :], Xb[h][cur][:], Yb[h][cur][:], True, True, [("Xb", cur, h), ("Yb", cur, h)], [gk])
                    mm(GBh[:, 1, :], Yb[h][cur][:], Xb[h][cur][:], True, True, [("Xb", cur, h), ("Yb", cur, h)], [gk])
                    yield
                    if lv <= 3:
                        cp("dve", Yb[h][nx][:], GBh[:, 0, :], [gk], [("Yb", nx, h)])
                    cp(ce_, Xb[h][nx][:], GBh[:, 1, :], [gk], [("Xb", nx, h)])
                    yield
                    mm(GBh[:, 2, :], Xb[h][nx][:], Rb[h][:], True, True, [("Xb", nx, h), K_("Rb")], [gk])
                    yield
                    tt("dve", Rb[h][:], GBh[:, 2, :], Rb[h][:], ALU.add, [K_("Rb"), gk], [K_("Rb")])
                    yield
                    cur = nx
                for Xl, kX in ((X1b[h], K_("X1b")), (X2b[h], K_("X2b"))):
                    mm(GBh[:, 0, :], Rb[h][:], identb[:], True, True, [K_("Rb"), "identb"], [gk])
                    mm(GBh[:, 1, :], Xl[:], Rb[h][:], True, True, [kX, K_("Rb")], [gk])
                    yield
                    cp(ce_, Tb[h][:], GBh[:, 0, :], [gk], [K_("Tb")])
                    cp("dve", Wb[h][:], GBh[:, 1, :], [gk], [K_("Wb")])
                    yield
                    mm(GBh[:, 2, :], Tb[h][:], Wb[h][:], True, True, [K_("Tb"), K_("Wb")], [gk])
                    yield
                    tt("dve", Rb[h][:], GBh[:, 2, :], Rb[h][:], ALU.add, [K_("Rb"), gk], [K_("Rb")])
                    yield
                mm(GBh[:, 0, :], Va, identb[:], True, True, [kV, "identb"], [gk])
                mm(GBh[:, 1, :], Ka, identb[:], True, True, [kK, "identb"], [gk])
                yield
                ts("dve", Vb[h][:], GBh[:, 0, :], btok[:, r_:r_ + 1], None, ALU.mult, None, [gk, "btok"], [K_("Vb")])
                ts("dve", Kb[h][:], GBh[:, 1, :], bek[:, r_:r_ + 1], None, ALU.mult, None, [gk, "bek"], [K_("Kb")])
                act(Kd[h][:], GBh[:, 1, :], AF.Identity, [gk, "ekd"], [K_("Kd")], scale=ekd[:, r_:r_ + 1])
                yield
                mm(GBh[:, 2, :], Rb[h][:], Vb[h][:], True, True, [K_("Rb"), K_("Vb")], [gk])
                mm(GBh[:, 3, :], Kb[h][:], Rb[h][:], True, True, [K_("Rb"), K_("Kb")], [gk])
                yield
                cp("act", Uf[h][:], GBh[:, 2, :], [gk], [K_("Uf")])
                cp("dve", WT[h][:], GBh[:, 3, :], [gk], [K_("WT")])
                yield
                mm(GBh[:, 0, :], WT[h][:], Sb[h][:], True, True, [K_("WT"), ("Sb", h)], [gk])
                mm(GBh[:, 1, :], Qa, Sb[h][:], True, True, [kQ, ("Sb", h)], [gk])
                yield
                stt("dve", vnew[h][:], GBh[:, 0, :], -1.0, Uf[h][:], ALU.mult, ALU.add, [K_("Uf"), gk], [K_("vnew")])
                act(Of[h][:], GBh[:, 1, :], AF.Identity, [gk, "egc"], [K_("Of")], scale=egc[:, r_:r_ + 1])
                yield
                mm(GBh[:, 2, :], PTm[h][:], vnew[h][:], True, True, [K_("PTm"), K_("vnew")], [gk])
                mm(GBh[:, 3, :], Kd[h][:], vnew[h][:], True, True, [K_("Kd"), K_("vnew")], [gk])
                yield
                tt("dve", Of[h][:], GBh[:, 2, :], Of[h][:], ALU.add, [K_("Of"), gk], [K_("Of")])
                ts("dve", Sf[h][:], Sf[h][:], egl[:, r_:r_ + 1], None, ALU.mult, None, [("Sf", h), "egl"], [("Sf", h)])
                tt("dve", Sf[h][:], GBh[:, 3, :], Sf[h][:], ALU.add, [("Sf", h), gk], [("Sf", h)])
                yield
                cp("act", Sb[h][:], Sf[h][:], [("Sf", h)], [("Sb", h)])
                mset("dve", ms[h][:], 0.0, [K_("ms")])
                act(junk128[:], Of[h][:], AF.Square, [K_("Of"), K_("ms")], ["junk128", K_("ms")], scale=128.0 ** -0.5, accum=ms[h][:])
                yield
                act(ms[h][:], ms[h][:], AF.Ln, [K_("ms"), "epsc"], [K_("ms")], bias=epsc[:])
                act(ms[h][:], ms[h][:], AF.Exp, [K_("ms")], [K_("ms")], scale=-0.5)
                yield
                ts("dve", onb[h][:], Of[h][:], ms[h][:], None, ALU.mult, None, [K_("Of"), K_("ms")], [K_("onb")])
                yield
                mm(GBh[:, 0, :], onb[h][:], identb[:], True, True, [K_("onb"), "identb"], [gk])
                yield
                stt("dve", oaT[:, h, bc], GBh[:, 0, :], ggdn[:], SZ[:, h, bc], ALU.mult, ALU.mult,
                    [gk, "ggdn", ("SZ", h)], [("oaT", h)])
                yield

        nkb = 4 * tt_ + 4
        noff = 4 * tt_

        def attn_gen():
            for h in range(4):
                for b in range(4):
                    col = (4 * tt_ + b) * 4 + h
                    ts("dve", Zz[:, b, b:b + 1], CTOK[:, col:col + 1], BASE[:, 16 * tt_ + h:16 * tt_ + h + 1], None, ALU.subtract, None,
                       ["CTOK", "BASE"], ["Zz"])
                for b in range(4):
                    mm(PN[0:4, b * 128:(b + 1) * 128], Zz[:, b, :], identf[:], True, True, ["Zz", "identf"], ["PN"])
                cp("dve", crx[0:4, :], PN[0:4, :], ["PN"], ["crx"])
                tt("dve", crh[:], PN[0:4, :], crx[0:4, :], ALU.subtract, ["PN", "crx"], ["crh"])
                cp("dve", crx[32:36, :], crh[:], ["crh"], ["crx"])
                P.add("dve", lambda e, h=h: e.tensor_scalar(
                    out=BT[:, 0:nkb], in0=CTOK[:, 0:nkb * 4].rearrange("p (k h) -> p k h", h=4)[:, :, h], scalar1=-1.0,
                    scalar2=REFM[:, 16 * tt_ + h:16 * tt_ + h + 1], op0=ALU.mult, op1=ALU.add), ["CTOK", "REFM"], ["BT"])
                yield
                if noff:
                    mm(PN[:], ones36[:], crx[:], True, True, ["ones36", "crx"], ["PN"])
                    act(ma[:], PN[:], AF.Exp, ["PN"], ["ma"])
                    yield
                blocks = []
                for kt in range(tt_ + 1):
                    for j in range(4):
                        blocks.append((kt, j))
                kvslot = {}

                def issue_kv(kt):
                    if kt > tt_ or kt in kvslot:
                        return
                    si_ = kt % 2
                    kvslot[kt] = si_
                    dma("sp", kst[si_][:], KT_d[h, :, kt * 512:(kt + 1) * 512], ["KT_d"], [("kst", si_)])
                    dma("sp", vst[si_][:], V1_d[kt * 4:(kt + 1) * 4].rearrange("b p n -> p b n"), ["V1_d"], [("vst", si_)])

                def emit_scores(bi):
                    kt, j = blocks[bi]
                    si_ = kvslot[kt]
                    diag = kt == tt_
                    c0 = j * 128 if diag else 0
                    i = bi % 2
                    mm(PM[i][:, c0:512], kst[si_][:, j * 128:(j + 1) * 128], QT[:, h, c0:512], True, not diag,
                       [("kst", si_), ("QT", h)], [PMK[i]])
                    if diag:
                        mm(PM[i][:, c0:512], ones36[:], crx[:, c0:512], False, True, ["ones36", "crx"], [PMK[i]])

                def emit_rest(bi):
                    kt, j = blocks[bi]
                    si_ = kvslot[kt]
                    kb = kt * 4 + j
                    diag = kt == tt_
                    c0 = j * 128 if diag else 0
                    i = bi % 2
                    pi = bi % 2
                    act(ptt[pi][:, c0:512], PM[i][:, c0:512], AF.Exp, [PMK[i], "BT"], [("ptt", pi)], bias=BT[:, kb:kb + 1])
                    if diag:
                        tt(PC, ptt[pi][:, c0:c0 + 128], ptt[pi][:, c0:c0 + 128], trib[:], ALU.mult, [("ptt", pi), "trib"], [("ptt", pi)])
                        st_, sp_ = (j == 0), (j == 3)
                    else:
                        st_, sp_ = (kb == 0), (kb == noff - 1)
                    mm(POT[:, c0:512], vst[si_][:, j, h * 129:h * 129 + 128], ptt[pi][:, c0:512], st_, sp_,
                       [("ptt", pi), ("vst", si_)], ["POT"])
                    mm(PN[:, c0:512], onesb[:], ptt[pi][:, c0:512], st_, sp_, [("ptt", pi), "onesb"], ["PN"])
                    if (not diag) and kb == noff - 1:
                        tt("dve", cacc[0][:], POT[:], ma[:], ALU.mult, ["POT", "ma"], [("cacc", 0)])
                        tt("dve", cacc[1][:], PN[:], ma[:], ALU.mult, ["PN", "ma"], [("cacc", 1)])

                issue_kv(0)
                issue_kv(1)
                emit_scores(0)
                for bi in range(len(blocks)):
                    if bi + 1 < len(blocks):
                        emit_scores(bi + 1)
                    emit_rest(bi)
                    if blocks[bi][1] == 3:
                        issue_kv(blocks[bi][0] + 2)
                    yield
                if noff:
                    tt("dve", etmp[1][:], PN[:], cacc[1][:], ALU.add, ["PN", ("cacc", 1)], [("etmp", 1)])
                    tt("dve", etmp[0][:], POT[:], cacc[0][:], ALU.add, ["POT", ("cacc", 0)], [("etmp", 0)])
                else:
                    cp("dve", etmp[1][:], PN[:], ["PN"], [("etmp", 1)])
                    cp("dve", etmp[0][:], POT[:], ["POT"], [("etmp", 0)])
                yield
                recip(etmp[1][:], etmp[1][:], [("etmp", 1)], [("etmp", 1)])
                tt("dve", etmp[0][:], etmp[0][:], etmp[1][:], ALU.mult, [("etmp", 0), ("etmp", 1)], [("etmp", 0)])
                tt("dve", ofT[:, h, :], etmp[0][:], SZ[:, 4 + h, :], ALU.mult, [("etmp", 0), ("SZ", 4 + h)], [("ofT", h)])
                yield

        gens = [gdn_head(h) for h in range(4)]
        ag = attn_gen()
        ratio = (12.0 + 16.0 * (tt_ + 1)) / 176.0
        accr = 0.0
        ag_live = True
        while gens:
            for g_ in list(gens):
                try:
                    next(g_)
                except StopIteration:
                    gens.remove(g_)
            accr += ratio
            while ag_live and accr >= 1.0:
                accr -= 1.0
                try:
                    next(ag)
                except StopIteration:
                    ag_live = False
        while ag_live:
            try:
                next(ag)
            except StopIteration:
                ag_live = False

        if stage == 6:
            raise _Stop()
        for g4 in range(4):
            wi = load_wgroup(6 + g4)
            for c in range(4):
                i = proj_chunk(wi, c)
                act(SG[:, c, :], PM[i][:], AF.Tanh, [PMK[i]], [("SG", c)], scale=0.5)
            for c2 in range(2):
                n = g4 * 2 + c2
                i = next_pm()
                for h in range(4):
                    mm(PM[i][:], wogb[:, h, n * 128:(n + 1) * 128], oaT[:, h, :], h == 0, h == 3, ["wogb", ("oaT", h)], [PMK[i]])
                stt("dve", ma[:], SG[:, c2, :], 1.0, PM[i][:], ALU.add, ALU.mult, [PMK[i], ("SG", c2)], ["ma"])
                i = next_pm()
                for h in range(4):
                    mm(PM[i][:], wofb[:, h, n * 128:(n + 1) * 128], ofT[:, h, :], h == 0, h == 3, ["wofb", ("ofT", h)], [PMK[i]])
                stt("dve", etmp[0][:], SG[:, 2 + c2, :], 1.0, PM[i][:], ALU.add, ALU.mult, [PMK[i], ("SG", 2 + c2)], [("etmp", 0)])
                tt("dve", mT[:, n, :], etmp[0][:], ma[:], ALU.add, [("etmp", 0), "ma"], [("mT", n)])
        if stage == 7:
            raise _Stop()
        for b in range(4):
            s_i = xi[0] % 2
            xi[0] += 1
            xb = stg[s_i]
            dma("sp", xb[:], x_d[t0 + b * 128:t0 + (b + 1) * 128, :], [], [("stg", s_i)])
            for n2 in range(2):
                i = next_pm()
                for k in range(8):
                    mm(PM[i][:], mT[:, k, b * 128:(b + 1) * 128], woutb[:, k, n2 * 512:(n2 + 1) * 512], k == 0, k == 7,
                       [("mT", k), "woutb"], [PMK[i]])
                tt("dve", xb[:, n2 * 512:(n2 + 1) * 512], PM[i][:], xb[:, n2 * 512:(n2 + 1) * 512], ALU.add,
                   [PMK[i], ("stg", s_i)], [("stg", s_i)])
            dma("pool", out_d[t0 + b * 128:t0 + (b + 1) * 128, :], xb[:], [("stg", s_i)], [("out", tt_, b)])

    try:
        for tt_ in range(NT):
            passB_tile(tt_)
    except _Stop:
        pass
    P.emit()
    st.close()
    return nc, P.n_ops


def host_inputs(inp, b):
    f = np.float32
    w_in = np.asarray(inp["w_in"], f)
    o = np.cumsum((0, 512, 512, 512, 4, 4, 512, 512, 512, 512, 4, 512, 1024, 1024))
    qa, ka, va, a_a, b_a, za, qf, kf, vf, f_f, zf, ga, gf = [w_in[:, o[i]:o[i + 1]] for i in range(13)]
    pairs = []
    for g in range(4):
        pairs += [ga[:, 2 * g * 128:(2 * g + 2) * 128], gf[:, 2 * g * 128:(2 * g + 2) * 128]]
    wB = np.ascontiguousarray(np.concatenate([qa, ka, va, za, qf, zf] + pairs, axis=1))
    c = np.asarray(inp["c"], f)[b]
    b_ada = np.asarray(inp["b_ada"], f)
    til = lambda v: np.ascontiguousarray(np.broadcast_to(np.tile(np.asarray(v, f), 4)[None, :], (128, 16)))
    col = lambda v: np.ascontiguousarray(np.asarray(v, f).reshape(128, 1))
    idx = np.arange(128)
    sel = np.zeros((16, 16, 128), f)
    for r in range(16):
        sel[r, r, :] = 1
    return {
        "x": np.ascontiguousarray(np.asarray(inp["x"], f)[b]),
        "cT": np.ascontiguousarray(c.reshape(8, 128).T),
        "w_ada": np.ascontiguousarray(np.asarray(inp["w_ada"], f)),
        "b_adaT": np.ascontiguousarray(b_ada[:2048].reshape(16, 128).T),
        "b_gate": np.ascontiguousarray(np.broadcast_to(b_ada[None, 2048:], (128, 1024))),
        "g_normT": np.ascontiguousarray(np.asarray(inp["g_norm"], f).reshape(8, 128).T),
        "wAk": np.ascontiguousarray(kf), "wAv": np.ascontiguousarray(vf), "wAf": np.ascontiguousarray(f_f),
        "wB": wB, "wab": np.ascontiguousarray(np.concatenate([a_a, b_a], axis=1)),
        "conv_wT": np.ascontiguousarray(np.asarray(inp["conv_w"], f).T.reshape(12, 128, 4).transpose(1, 0, 2)),
        "alog_bc": til(inp["A_log"]), "dtb_bc": til(inp["dt_bias"]), "bf_bc": til(inp["b_f"]),
        "g_gdn": col(inp["g_gdn_out"]), "g_q": col(inp["g_q_fox"]), "g_k": col(inp["g_k_fox"]),
        "g_q_row": np.ascontiguousarray(np.asarray(inp["g_q_fox"], f).reshape(1, 128)),
        "g_k_row": np.ascontiguousarray(np.asarray(inp["g_k_fox"], f).reshape(1, 128)),
        "w_o_gdn": np.ascontiguousarray(np.asarray(inp["w_o_gdn"], f)),
        "w_o_fox": np.ascontiguousarray(np.asarray(inp["w_o_fox"], f)),
        "w_out": np.ascontiguousarray(np.asarray(inp["w_out"], f)),
        "ident": np.eye(128, dtype=f),
        "tri": (idx[:, None] <= idx[None, :]).astype(f),
        "ml0": ((idx[None, :] < idx[:, None]) & (idx[None, :] // 32 == idx[:, None] // 32)).astype(f),
        "ml1": ((idx[None, :] < idx[:, None]) & (idx[None, :] // 32 != idx[:, None] // 32)
                & (idx[None, :] // 64 == idx[:, None] // 64)).astype(f),
        "ml2": ((idx[None, :] < idx[:, None]) & (idx[None, :] // 64 != idx[:, None] // 64)).astype(f),
        "sel4": np.ascontiguousarray(np.kron(np.eye(4, dtype=f), np.ones((1, 128), f))),
    }


def kernel(**inputs):
    x = np.asarray(inputs["x"])
    B, T, _ = x.shape
    nc, _ = build(T)
    in_maps = [host_inputs(inputs, b) for b in range(B)]
    res = run_bass_kernel_spmd(nc, in_maps, core_ids=list(range(B)))
    return np.stack([np.asarray(r["out"], np.float32) for r in res.results], axis=0)
```

```python
import numpy as np
from contextlib import ExitStack
import concourse.bass as bass
import concourse.mybir as mybir
from concourse.bass_utils import run_bass_kernel_spmd

F32 = mybir.dt.float32
BF16 = mybir.dt.bfloat16
AF = mybir.ActivationFunctionType
ALU = mybir.AluOpType
NDSEM = 6
CE_T = 6
D = 1024
EPS = 1e-6


class _Stop(Exception):
    pass


class Prog:
    def __init__(self, nc):
        self.nc = nc
        self.ops = []

    limit = None

    excl = ("POT", "PN", "G0", "G1", ("PM", 0), ("PM", 1), ("PO", 0), ("PO", 1))

    def add(self, eng, fn, r=(), w=(), dma=False):
        if self.limit is not None and len(self.ops) >= self.limit:
            raise _Stop()
        w = tuple(w) + tuple(k for k in r if k in self.excl and k not in w)
        self.ops.append((eng, fn, tuple(r), tuple(w), dma))

    def pe(self, fn, r=(), w=()):
        self.add("pe", fn, r, w)

    def dma(self, q, fn, r=(), w=()):
        self.add(q, fn, r, w, dma=True)

    def emit(self):
        nc = self.nc
        ops = self.ops
        n = len(ops)
        last_w = {}
        readers = {}
        deps = [None] * n
        for i, (eng, fn, r, w, dma) in enumerate(ops):
            d = set()
            for k in r:
                if k in last_w:
                    d.add(last_w[k])
            for k in w:
                if k in last_w:
                    d.add(last_w[k])
                d.update(readers.get(k, ()))
            d.discard(i)
            deps[i] = d
            for k in r:
                readers.setdefault(k, []).append(i)
            for k in w:
                last_w[k] = i
                readers[k] = []
        needs_sig = [False] * n
        for i in range(n):
            for j in deps[i]:
                if ops[j][4]:
                    continue
                if ops[j][0] == "pe" and ops[i][0] == "pe" and not ops[i][4]:
                    continue
                needs_sig[j] = True
        engs = sorted(set(o[0] for o in ops))
        cnt = {e: 0 for e in engs}
        sig = [None] * n
        dma_cnt = {e: 0 for e in engs}
        dma_prev = {}
        pre_wait = [None] * n
        for i, (eng, fn, r, w, dma) in enumerate(ops):
            if dma:
                slot = dma_cnt[eng] % NDSEM
                dma_cnt[eng] += 1
                prev = dma_prev.get((eng, slot), 0)
                if prev:
                    pre_wait[i] = (("d", eng, slot), prev)
                dma_prev[(eng, slot)] = prev + 16
                sig[i] = (("d", eng, slot), prev + 16)
            elif needs_sig[i]:
                cnt[eng] += 1
                sig[i] = (("c", eng), cnt[eng])
        semkeys = set(s_[0] for s_ in sig if s_ is not None)
        stack = ExitStack()
        sems = {}
        for k in sorted(semkeys, key=str):
            sems[k] = stack.enter_context(nc.semaphore("s_" + "_".join(map(str, k))))
        seen = {e: {} for e in engs}
        plan = {e: [] for e in engs}
        last_dma = {}
        for i, (eng, fn, r, w, dma) in enumerate(ops):
            waits = {}
            if pre_wait[i] is not None:
                k, v = pre_wait[i]
                waits[k] = max(waits.get(k, 0), v)
            for j in deps[i]:
                if sig[j] is None:
                    continue
                k, v = sig[j]
                if k == ("c", "pe") and eng == "pe" and not dma:
                    continue
                waits[k] = max(waits.get(k, 0), v)
            wl = []
            for k, v in waits.items():
                if seen[eng].get(k, 0) >= v:
                    continue
                seen[eng][k] = v
                wl.append((k, v))
            plan[eng].append((wl, fn, sig[i], dma))
            if dma:
                last_dma[sig[i][0]] = sig[i][1]
        self.n_ops = n
        block = stack.enter_context(nc.Block())

        def mk(eng):
            def body(e):
                for wl, fn, sg, dma in plan[eng]:
                    for k, v in wl:
                        e.wait_ge(sems[k], v)
                    ins = fn(e)
                    if sg is not None:
                        ins.then_inc(sems[sg[0]], 16 if dma else 1)
                for k, v in last_dma.items():
                    if k[1] == eng:
                        e.wait_ge(sems[k], v)
            return body

        reg = {"pe": block.tensor, "act": block.scalar, "dve": block.vector,
               "pool": block.gpsimd, "sp": block.sync}
        for eng in engs:
            reg[eng](mk(eng))
        stack.close()


def build(T, stage=9, dq='pool', PC='pool', limit=None):
    NT = T // 512
    NBLK = T // 128
    nc = bass.Bass("TRN2", target_bir_lowering=False)
    di = lambda name, shape, dt=F32: nc.dram_tensor(name, list(shape), dt, kind="ExternalInput").ap()
    x_d = di("x", [T, D])
    cT_d = di("cT", [128, 8])
    wada_d = di("w_ada", [D, 3 * D])
    badaT_d = di("b_adaT", [128, 16])
    bgate_d = di("b_gate", [128, D])
    gnT_d = di("g_normT", [128, 8])
    wAk_d = di("wAk", [D, 512])
    wAv_d = di("wAv", [D, 512])
    wAf_d = di("wAf", [D, 4])
    wB_d = di("wB", [D, 5120])
    wab_d = di("wab", [D, 8])
    convT_d = di("conv_wT", [128, 12, 4])
    alog_d = di("alog_bc", [128, 16])
    dtb_d = di("dtb_bc", [128, 16])
    bf_d = di("bf_bc", [128, 16])
    ggdn_d = di("g_gdn", [128, 1])
    gq_d = di("g_q", [128, 1])
    gk_d = di("g_k", [128, 1])
    gqr_d = di("g_q_row", [1, 128])
    gkr_d = di("g_k_row", [1, 128])
    wog_d = di("w_o_gdn", [512, D])
    wof_d = di("w_o_fox", [512, D])
    wout_d = di("w_out", [D, D])
    ident_d = di("ident", [128, 128])
    tri_d = di("tri", [128, 128])
    ml0_d = di("ml0", [128, 128])
    ml1_d = di("ml1", [128, 128])
    ml2_d = di("ml2", [128, 128])
    sel_d = di("sel4", [4, 512])
    out_d = nc.dram_tensor("out", [T, D], F32, kind="ExternalOutput").ap()
    wBbf_d = nc.dram_tensor("wB_bf", [D, 5120], BF16).ap()
    KT_d = nc.dram_tensor("KT_s", [4, 128, T], BF16).ap()
    V1_d = nc.dram_tensor("V1_s", [NBLK, 128, 516], BF16).ap()

    st = ExitStack()
    sb = lambda name, shape, dt=F32: st.enter_context(nc.sbuf_tensor(name, list(shape), dt))
    ps = lambda name, shape, dt=F32: st.enter_context(nc.psum_tensor(name, list(shape), dt))
    P = Prog(nc)
    P.limit = limit

    def dma(q, out, in_, r, w):
        if q == "pool":
            q = dq
        P.dma(q, lambda e: e.dma_start(out=out, in_=in_), r, w)

    def mm(out, lhsT, rhs, start, stop, r, w):
        P.pe(lambda e: e.matmul(out, lhsT=lhsT, rhs=rhs, start=start, stop=stop), r, w)

    def tr(out, in_, idn, r, w):
        P.pe(lambda e: e.transpose(out=out, in_=in_, identity=idn), r, w)

    def act(out, in_, func, r, w, scale=1.0, bias=None, accum=None, eng="act"):
        kw = {}
        if bias is not None:
            kw["bias"] = bias
        if accum is not None:
            kw["accum_out"] = accum
        P.add(eng, lambda e: e.activation(out=out, in_=in_, func=func, scale=scale, **kw), r, w)

    def ts(eng, out, in0, s1, s2, op0, op1, r, w):
        if s2 is None:
            P.add(eng, lambda e: e.tensor_scalar(out=out, in0=in0, scalar1=s1, scalar2=None, op0=op0), r, w)
        else:
            P.add(eng, lambda e: e.tensor_scalar(out=out, in0=in0, scalar1=s1, scalar2=s2, op0=op0, op1=op1), r, w)

    def tt(eng, out, in0, in1, op, r, w):
        P.add(eng, lambda e: e.tensor_tensor(out=out, in0=in0, in1=in1, op=op), r, w)

    def stt(eng, out, in0, scalar, in1, op0, op1, r, w):
        P.add(eng, lambda e: e.scalar_tensor_tensor(out=out, in0=in0, scalar=scalar, in1=in1, op0=op0, op1=op1), r, w)

    def cp(eng, out, in_, r, w):
        if eng == "act":
            P.add(eng, lambda e: e.activation(out=out, in_=in_, func=AF.Copy), r, w)
        else:
            P.add(eng, lambda e: e.tensor_copy(out=out, in_=in_), r, w)

    def mset(eng, ap, val, w):
        P.add(eng, lambda e: e.memset(ap, val), (), w)

    def recip(out, in_, r, w):
        P.add("dve", lambda e: e.reciprocal(out=out, in_=in_), r, w)

    identf = sb("identf", [128, 128]); trif = sb("trif", [128, 128]); ml0f = sb("ml0f", [128, 128]); ml1f = sb("ml1f", [128, 128]); ml2f = sb("ml2f", [128, 128])
    onesf = sb("onesf", [128, 128]); identb = sb("identb", [128, 128], BF16); onesb = sb("onesb", [128, 128], BF16)
    sel4 = sb("sel4s", [4, 512])
    onec = sb("onec", [128, 1]); epsc = sb("epsc", [128, 1])
    cT = sb("cTs", [128, 8]); badaT = sb("badaTs", [128, 16]); gnT = sb("gnTs", [128, 8])
    aT = sb("aT", [128, 8]); shT = sb("shT", [128, 8]); modT = sb("modT", [128, 16])
    convw = sb("convw", [128, 12, 4]); negA = sb("negA", [128, 16]); dtb = sb("dtb", [128, 16]); bfb = sb("bfb", [128, 16])
    ggdn = sb("ggdn", [128, 1]); gq = sb("gqs", [128, 1]); gk = sb("gks", [128, 1]); gqs = sb("gqsc", [128, 1])
    grow = sb("grow", [1, 256]); grow2 = sb("grow2", [1, 256]); gmx = sb("gmx", [1, 2]); mb = sb("mb", [128, 1]); negm = sb("negm", [128, 1])
    woutb = sb("woutb", [128, 8, D], BF16); wogb = sb("wogb", [128, 4, D], BF16); wofb = sb("wofb", [128, 4, D], BF16)
    wabb = sb("wabb", [128, 8, 8], BF16); wAfb = sb("wAfb", [128, 8, 4], BF16)
    wg = [sb("wg%d" % i, [128, 8, 512], BF16) for i in range(2)]
    stg = [sb("stg%d" % i, [128, 1024]) for i in range(2)]
    stgb = [sb("stgb%d" % i, [128, 512], BF16) for i in range(2)]
    hnb = [sb("hn%d" % i, [128, D], BF16) for i in range(2)]
    ssq = sb("ssq", [128, 4]); rstd = sb("rstd", [128, 4])
    hT = sb("hT", [128, 8, 512], BF16)
    sq = [sb("sq%d" % i, [128, 512], BF16) for i in range(2)]; rst = [sb("rst%d" % i, [128, 512]) for i in range(2)]
    ktile = [sb("ktile%d" % i, [128, 512], BF16) for i in range(2)]
    v1blk = [sb("v1blk%d" % i, [128, 4, 129], BF16) for i in range(2)]
    LOGF = sb("LOGF", [128, NBLK * 4]); CTOK = sb("CTOK", [128, NBLK * 4]); BASE = sb("BASE", [128, NBLK * 4])
    TOTS = sb("TOTS", [128, NBLK * 4]); REFM = sb("REFM", [128, NBLK * 4])
    tmp16 = sb("tmp16", [128, 16]); tmp16b = sb("tmp16b", [128, 16])
    U = sb("U", [128, 4, 515]); halo = sb("halo", [128, 12, 3])
    cBv = lambda k: U[:, k // 4, (k % 4) * 128:(k % 4 + 1) * 128]
    cBk = lambda k: ("U", k // 4)
    gatev = lambda n2: U[:, 2 + n2, 0:512]
    gatek = lambda n2: ("U", 2 + n2)
    cacc = [sb("cacc%d" % i, [128, 512]) for i in range(2)]
    etmp = [sb("etmp%d" % i, [128, 512]) for i in range(2)]
    QKV = sb("QKV", [128, 12, 512], BF16)
    SZ = sb("SZ", [128, 8, 512], BF16); QT = sb("QT", [128, 4, 512], BF16)
    SG = sb("SG", [128, 4, 512], BF16)
    oaT = sb("oaT", [128, 4, 512], BF16); ofT = sb("ofT", [128, 4, 512], BF16)
    mT = sb("mT", [128, 8, 512], BF16); ma = sb("ma", [128, 512])
    gtok = sb("gtok", [128, 16]); btok = sb("btok", [128, 16]); nbeta = sb("nbeta", [128, 16])
    gc = sb("gc", [128, 16]); gsum = sb("gsum", [128, 16]); egc = sb("egc", [128, 16]); egl = sb("egl", [128, 16])
    ekd = sb("ekd", [128, 16]); bek = sb("bek", [128, 16]); gcrow4 = sb("gcrow4", [4, 512])
    H4 = range(4)
    mkf = lambda nm: [sb("%s_%d" % (nm, h), [128, 128]) for h in H4]
    mkb = lambda nm: [sb("%s_%d" % (nm, h), [128, 128], BF16) for h in H4]
    t1 = mkf("t1"); t2 = mkf("t2"); x1 = mkf("x1"); pt1 = mkf("pt1"); Uf = mkf("Uf"); Of = mkf("Of")
    Xb = [[sb("Xb%d_%d" % (i, h), [128, 128], BF16) for i in range(2)] for h in H4]
    Yb = [[sb("Yb%d_%d" % (i, h), [128, 128], BF16) for i in range(2)] for h in H4]
    Rb = mkb("Rb"); Tb = mkb("Tb"); Wb = mkb("Wb"); X1b = mkb("X1b"); X2b = mkb("X2b"); PTm = mkb("PTm")
    Vb = mkb("Vb"); Kb = mkb("Kb"); Kd = mkb("Kd"); WT = mkb("WT"); vnew = mkb("vnew"); onb = mkb("onb")
    ms = [sb("ms_%d" % h, [128, 1]) for h in H4]; junk128 = sb("junk128", [128, 128], BF16)
    Sf = [sb("Sf%d" % h, [128, 128]) for h in range(4)]
    Sb = [sb("Sb%d" % h, [128, 128], BF16) for h in range(4)]
    crt = sb("crt", [128, 4]); oden = sb("oden", [128, 4]); Zz = sb("Zz", [128, 4, 4]); crx = sb("crx", [36, 512], BF16); crh = sb("crh", [4, 512])
    ones36 = sb("ones36", [36, 128], BF16)
    BT = sb("BT", [128, NBLK])
    kst = [sb("kst%d" % i, [128, 512], BF16) for i in range(2)]
    vst = [sb("vst%d" % i, [128, 4, 516], BF16) for i in range(2)]
    ptt = [sb("ptt%d" % i, [128, 512], BF16) for i in range(2)]
    trib = sb("trib", [128, 128], BF16)
    rl = sb("rl", [128, 1]); onf = sb("onf", [128, 128], BF16)

    POT = ps("POT", [128, 512])
    PM = [ps("PM%d" % i, [128, 512]) for i in range(2)]
    PN = ps("PN", [128, 512])
    G0 = ps("G0", [128, 4, 128]); G1 = ps("G1", [128, 4, 128])
    PO = [ps("PO%d" % i, [128, 4, 128]) for i in range(2)]

    PM = [PM[0][:], PM[1][:]] + [X_[:].rearrange("p a b -> p (a b)") for X_ in (G0, G1, PO[0], PO[1])]
    PMK = [("PM", 0), ("PM", 1), "G0", "G1", ("PO", 0), ("PO", 1)]
    POa = [X_[:].rearrange("p a b -> p (a b)")[:, 0:129] for X_ in (PO[0], PO[1], G0, G1)]
    GB = [G0, G1, PO[0], PO[1]]
    GBk = ["G0", "G1", ("PO", 0), ("PO", 1)]
    POk = [("PO", 0), ("PO", 1), "G0", "G1"]
    for name, dst, src in (("identf", identf, ident_d), ("trif", trif, tri_d), ("ml0f", ml0f, ml0_d), ("ml1f", ml1f, ml1_d), ("ml2f", ml2f, ml2_d),
                           ("sel4", sel4, sel_d), ("cT", cT, cT_d), ("badaT", badaT, badaT_d), ("gnT", gnT, gnT_d),
                           ("convw", convw, convT_d), ("negA", negA, alog_d), ("dtb", dtb, dtb_d), ("bfb", bfb, bf_d),
                           ("ggdn", ggdn, ggdn_d), ("gq", gq, gq_d), ("gk", gk, gk_d)):
        dma("sp", dst[:], src, [], [name])
    for n2 in range(2):
        dma("sp", gatev(n2), bgate_d[:, n2 * 512:(n2 + 1) * 512], [], [gatek(n2)])
    dma("sp", grow[:, 0:128], gqr_d, [], ["grow"])
    dma("sp", grow[:, 128:256], gkr_d, [], ["grow"])
    mset("dve", onesf[:], 1.0, ["onesf"]); mset("dve", onesb[:], 1.0, ["onesb"]); mset("dve", onec[:], 1.0, ["onec"])
    mset("dve", epsc[:], EPS, ["epsc"]); mset("dve", ones36[:], 1.0, ["ones36"]); mset("dve", crx[:], 0.0, ["crx"])
    mset("dve", Zz[:], 0.0, ["Zz"]); mset("dve", halo[:], 0.0, ["halo"])
    for i in range(2):
        mset("dve", v1blk[i][:], 1.0, [("v1blk", i)])
    for h in range(4):
        mset("dve", Sf[h][:], 0.0, [("Sf", h)]); mset("dve", Sb[h][:], 0.0, [("Sb", h)])
    cp("dve", identb[:], identf[:], ["identf"], ["identb"])
    cp("dve", trib[:], trif[:], ["trif"], ["trib"])
    ts("dve", convw[:], convw[:], 0.5, None, ALU.mult, None, ["convw"], ["convw"])
    ts("dve", ggdn[:], ggdn[:], 0.5, None, ALU.mult, None, ["ggdn"], ["ggdn"])
    act(negA[:], negA[:], AF.Exp, ["negA"], ["negA"])
    ts("dve", negA[:], negA[:], -1.0, None, ALU.mult, None, ["negA"], ["negA"])
    ts("dve", gqs[:], gq[:], 128.0 ** -0.5, None, ALU.mult, None, ["gq"], ["gqs"])
    ts("dve", grow2[:], grow[:], -1.0, None, ALU.mult, None, ["grow"], ["grow2"])
    tt("dve", grow[:], grow[:], grow2[:], ALU.max, ["grow", "grow2"], ["grow"])
    P.add("dve", lambda e: e.reduce_max(out=gmx[:, 0:1], in_=grow[:, 0:128], axis=mybir.AxisListType.X), ["grow"], ["gmx"])
    P.add("dve", lambda e: e.reduce_max(out=gmx[:, 1:2], in_=grow[:, 128:256], axis=mybir.AxisListType.X), ["grow"], ["gmx"])
    ts("dve", gmx[:, 0:1], gmx[:, 0:1], gmx[:, 1:2], -(128.0 ** 0.5), ALU.mult, ALU.mult, ["gmx"], ["gmx"])
    mm(PN[:, 0:1], onesf[0:1, :], gmx[:, 0:1], True, True, ["onesf", "gmx"], ["PN"])
    cp("dve", negm[:], PN[:, 0:1], ["PN"], ["negm"])

    for j in range(16):
        s_ = stg[j % 2]
        dma("sp", s_[:].rearrange("p (k n) -> p k n", k=8), wada_d[:, j * 128:(j + 1) * 128].rearrange("(k p) n -> p k n", p=128),
            [], [("stg", j % 2)])
        for k in range(8):
            mm(PN[:, 16 + j:17 + j], s_[:, k * 128:(k + 1) * 128], cT[:, k:k + 1], k == 0, k == 7,
               [("stg", j % 2), "cT"], ["PN"])
    for j in range(16):
        tt("dve", modT[:, j:j + 1], PN[:, 16 + j:17 + j], badaT[:, j:j + 1], ALU.add, ["PN", "badaT"], ["modT"])
    cp("dve", shT[:], modT[:, 0:8], ["modT"], ["shT"])
    stt("dve", aT[:], modT[:, 8:16], 1.0, gnT[:], ALU.add, ALU.mult, ["modT", "gnT"], ["aT"])
    for k in range(8):
        ts("dve", cBv(k), onesf[:], cT[:, k:k + 1], None, ALU.mult, None, ["onesf", "cT"], [cBk(k)])
    for k in range(8):
        s_ = stg[k % 2]
        dma("sp", s_[:], wada_d[k * 128:(k + 1) * 128, 2048:3072], [], [("stg", k % 2)])
        for n2 in range(2):
            mm(PM[n2][:], cBv(k), s_[:, n2 * 512:(n2 + 1) * 512], k == 0, k == 7, [cBk(k), ("stg", k % 2)], [("PM", n2)])
    for n2 in range(2):
        tt("dve", gatev(n2), PM[n2][:], gatev(n2), ALU.add, [("PM", n2), gatek(n2)], [gatek(n2)])
    si = 0
    for k in range(8):
        s_ = stg[si % 2]
        dma("sp", s_[:], wout_d[k * 128:(k + 1) * 128, :], [], [("stg", si % 2)])
        for n2 in range(2):
            stt("dve", woutb[:, k, n2 * 512:(n2 + 1) * 512], s_[:, n2 * 512:(n2 + 1) * 512], 0.5, gatev(n2), ALU.mult, ALU.mult,
                [("stg", si % 2), gatek(n2)], ["woutb"])
        si += 1
    for (wd, wb_, nm) in ((wog_d, wogb, "wogb"), (wof_d, wofb, "wofb")):
        for k in range(4):
            s_ = stg[si % 2]
            dma("sp", s_[:], wd[k * 128:(k + 1) * 128, :], [], [("stg", si % 2)])
            act(wb_[:, k, :], s_[:], AF.Identity, [("stg", si % 2)], [nm], scale=(0.5 if nm == "wofb" else 1.0))
            si += 1
    s_ = stg[si % 2]
    dma("sp", s_[:, 0:64].rearrange("p (k n) -> p k n", k=8), wab_d.rearrange("(k p) n -> p k n", p=128), [], [("stg", si % 2)])
    cp("dve", wabb[:], s_[:, 0:64].rearrange("p (k n) -> p k n", k=8), [("stg", si % 2)], ["wabb"])
    si += 1
    s_ = stg[si % 2]
    dma("sp", s_[:, 0:32].rearrange("p (k n) -> p k n", k=8), wAf_d.rearrange("(k p) n -> p k n", p=128), [], [("stg", si % 2)])
    cp("dve", wAfb[:], s_[:, 0:32].rearrange("p (k n) -> p k n", k=8), [("stg", si % 2)], ["wAfb"])
    si += 1
    for k in range(8):
        for g in range(5):
            s_ = stg[si % 2]
            dma("sp", s_[:], wB_d[k * 128:(k + 1) * 128, g * 1024:(g + 1) * 1024], [], [("stg", si % 2)])
            for hh in range(2):
                cp("act" if hh else "dve", stgb[hh][:], s_[:, hh * 512:(hh + 1) * 512], [("stg", si % 2)], [("stgb", hh)])
                dma("pool", wBbf_d[k * 128:(k + 1) * 128, g * 1024 + hh * 512:g * 1024 + (hh + 1) * 512], stgb[hh][:],
                    [("stgb", hh)], ["wBbf"])
            si += 1
    for (wd, wi) in ((wAk_d, 0), (wAv_d, 1)):
        for k in range(8):
            s_ = stg[si % 2]
            dma("sp", s_[:, 0:512], wd[k * 128:(k + 1) * 128, :], [], [("stg", si % 2)])
            cp("act", wg[wi][:, k, :], s_[:, 0:512], [("stg", si % 2)], [("wg", wi)])
            si += 1

    if stage == 0:
        P.emit(); st.close(); return nc, P.n_ops
    xi = [0]

    def make_hT(tt_):
        t0 = tt_ * 512
        for b in range(4):
            s_i = xi[0] % 2
            xi[0] += 1
            xb = stg[s_i]
            hb = hnb[b % 2]
            kh = ("hn", b % 2)
            dma("sp", xb[:], x_d[t0 + b * 128:t0 + (b + 1) * 128, :], [], [("stg", s_i)])
            mset("dve", ssq[:, b:b + 1], 0.0, [("ssq", b)])
            act(hb[:], xb[:], AF.Square, [("stg", s_i), ("ssq", b)], [kh, ("ssq", b)], scale=1.0 / 32, accum=ssq[:, b:b + 1])
            act(rstd[:, b:b + 1], ssq[:, b:b + 1], AF.Ln, [("ssq", b), "epsc"], [("rstd", b)], bias=epsc[:])
            act(rstd[:, b:b + 1], rstd[:, b:b + 1], AF.Exp, [("rstd", b)], [("rstd", b)], scale=-0.5)
            ts("dve", hb[:], xb[:], rstd[:, b:b + 1], None, ALU.mult, None, [("stg", s_i), ("rstd", b)], [kh])
            for k4 in range(2):
                i = next_pm()
                for k in range(k4 * 4, k4 * 4 + 4):
                    mm(PM[i][:, (k % 4) * 128:(k % 4 + 1) * 128], hb[:, k * 128:(k + 1) * 128], identb[:], True, True,
                       [kh, "identb"], [PMK[i]])
                for k in range(k4 * 4, k4 * 4 + 4):
                    src = PM[i][:, (k % 4) * 128:(k % 4 + 1) * 128]
                    if k % 2 == 0:
                        act(hT[:, k, b * 128:(b + 1) * 128], src, AF.Identity, [PMK[i], "aT", "shT"], [("hT", k)],
                            scale=aT[:, k:k + 1], bias=shT[:, k:k + 1])
                    else:
                        ts("dve", hT[:, k, b * 128:(b + 1) * 128], src, aT[:, k:k + 1], shT[:, k:k + 1], ALU.mult, ALU.add,
                           [PMK[i], "aT", "shT"], [("hT", k)])

    pmi = [0]
    nrm_n = [0]

    def norm_gen(src_fn, out_ap, out_keys, mean_scale, scalar, scalar_keys, post=None):
        src_ap, src_keys = src_fn()
        j = nrm_n[0] % 2
        nrm_n[0] += 1
        act(sq[j][:], src_ap, AF.Square, src_keys, [("sq", j)])
        yield
        i = next_pm()
        mm(PM[i][:], onesb[:], sq[j][:], True, True, ["onesb", ("sq", j)], [PMK[i]])
        yield
        act(rst[j][:], PM[i][:], AF.Ln, [PMK[i], "epsc"], [("rst", j)], scale=mean_scale, bias=epsc[:])
        yield
        act(rst[j][:], rst[j][:], AF.Exp, [("rst", j)], [("rst", j)], scale=-0.5)
        yield
        stt("dve", out_ap, src_ap, scalar, rst[j][:], ALU.mult, ALU.mult, src_keys + scalar_keys + [("rst", j)], out_keys)
        if post is not None:
            post()

    def run_staggered(pend, maxlive=2):
        gens = []
        while gens or pend:
            if pend and len(gens) < maxlive:
                gens.append(pend.pop(0))
            for g_ in list(gens):
                try:
                    next(g_)
                except StopIteration:
                    gens.remove(g_)

    def next_pm():
        i = pmi[0] % 6
        pmi[0] += 1
        return i

    for tt_ in range(NT):
        t0 = tt_ * 512
        make_hT(tt_)
        def k_src(h):
            def f():
                i = next_pm()
                for k in range(8):
                    mm(PM[i][:], wg[0][:, k, h * 128:(h + 1) * 128], hT[:, k, :], k == 0, k == 7, [("wg", 0), ("hT", k)], [PMK[i]])
                return PM[i][:], [PMK[i]]
            return f

        def k_post(h):
            return lambda: dma("pool", KT_d[h, :, t0:t0 + 512], ktile[h % 2][:], [("ktile", h % 2)], ["KT_d"])

        run_staggered([norm_gen(k_src(h), ktile[h % 2][:], [("ktile", h % 2)], 1.0 / 128, gk[:], ["gk"], post=k_post(h))
                       for h in range(4)])
        for b in range(4):
            i = next_pm()
            vi = b % 2
            for k in range(8):
                mm(PM[i][:], hT[:, k, b * 128:(b + 1) * 128], wg[1][:, k, :], k == 0, k == 7, [("wg", 1), ("hT", k)], [PMK[i]])
            cp("act", v1blk[vi][:, :, 0:128], PM[i][:].rearrange("p (h d) -> p h d", h=4), [PMK[i]], [("v1blk", vi)])
            dma("pool", V1_d[tt_ * 4 + b], v1blk[vi][:].rearrange("p h d -> p (h d)"), [("v1blk", vi)], ["V1_d"])
            for k in range(8):
                mm(PN[:, b * 4:(b + 1) * 4], hT[:, k, b * 128:(b + 1) * 128], wAfb[:, k, :], k == 0, k == 7,
                   ["wAfb", ("hT", k)], ["PN"])
        tt("dve", tmp16[:], PN[:, 0:16], bfb[:], ALU.add, ["PN", "bfb"], ["tmp16"])
        act(tmp16[:], tmp16[:], AF.Exp, ["tmp16"], ["tmp16"], scale=-1.0)
        act(tmp16[:], tmp16[:], AF.Ln, ["tmp16", "onec"], ["tmp16"], bias=onec[:])
        ts("dve", LOGF[:, tt_ * 16:(tt_ + 1) * 16], tmp16[:], -1.0, None, ALU.mult, None, ["tmp16"], ["LOGF"])

    NB4 = NBLK * 4
    for c0 in range(0, NB4, 512):
        c1 = min(NB4, c0 + 512)
        mm(PM[0][:, 0:c1 - c0], trif[:], LOGF[:, c0:c1], True, True, ["trif", "LOGF"], [("PM", 0)])
        mm(PM[1][:, 0:c1 - c0], onesf[:], LOGF[:, c0:c1], True, True, ["onesf", "LOGF"], [("PM", 1)])
        cp("dve", CTOK[:, c0:c1], PM[0][:, 0:c1 - c0], [("PM", 0)], ["CTOK"])
        cp("dve", TOTS[:, c0:c1], PM[1][:, 0:c1 - c0], [("PM", 1)], ["TOTS"])
    mset("dve", BASE[:, 0:4], 0.0, ["BASE"])
    for bl in range(1, NBLK):
        tt("dve", BASE[:, bl * 4:(bl + 1) * 4], BASE[:, (bl - 1) * 4:bl * 4], TOTS[:, (bl - 1) * 4:bl * 4], ALU.add,
           ["BASE", "TOTS"], ["BASE"])
    tt("dve", CTOK[:], CTOK[:], BASE[:], ALU.add, ["CTOK", "BASE"], ["CTOK"])
    ts("dve", REFM[:], BASE[:], negm[:], None, ALU.add, None, ["BASE", "negm"], ["REFM"])

    if stage == 1:
        P.emit(); st.close(); return nc, P.n_ops
    wgi = [0]
    kvi = [0]
    G0k = lambda s: "G0"
    G1k = lambda s: "G1"

    WORDER = [0, 1, 2, 3, 5, 4, 6, 7, 8, 9]
    wpending = {}

    def _issue_w(seq):
        i = seq % 2
        g = WORDER[seq % 10]
        dma("sp", wg[i][:], wBbf_d[:, g * 512:(g + 1) * 512].rearrange("(k p) n -> p k n", p=128),
            ["wBbf"], [("wg", i)])
        wpending[seq] = i

    def load_wgroup(g):
        seq = wgi[0]
        wgi[0] += 1
        assert WORDER[seq % 10] == g
        if seq not in wpending:
            _issue_w(seq)
        i = wpending.pop(seq)
        if seq + 1 < 10 * NT:
            wnext[0] = seq + 1
        return i

    wnext = [None]

    def prefetch_w():
        if wnext[0] is not None and wnext[0] not in wpending:
            _issue_w(wnext[0])
        wnext[0] = None

    def proj_chunk(wi, c):
        i = next_pm()
        for k in range(8):
            mm(PM[i][:], wg[wi][:, k, c * 128:(c + 1) * 128], hT[:, k, :], k == 0, k == 7, [("wg", wi), ("hT", k)], [PMK[i]])
        if c == 0:
            prefetch_w()
        return i

    def silu_from(src_ap, src_keys, out_ap, out_keys, ei):
        e_ = etmp[ei]
        act(e_[:], src_ap, AF.Exp, src_keys, [("etmp", ei)], scale=-1.0)
        ts("dve", e_[:], e_[:], 1.0, None, ALU.add, None, [("etmp", ei)], [("etmp", ei)])
        recip(e_[:], e_[:], [("etmp", ei)], [("etmp", ei)])
        tt("dve", out_ap, src_ap, e_[:], ALU.mult, src_keys + [("etmp", ei)], out_keys)

    def passB_tile(tt_):
        t0 = tt_ * 512
        make_hT(tt_)
        for b in range(4):
            for k in range(8):
                mm(PN[:, 32 + b * 8:40 + b * 8], hT[:, k, b * 128:(b + 1) * 128], wabb[:, k, :], k == 0, k == 7,
                   ["wabb", ("hT", k)], ["PN"])
        if stage == 2:
            raise _Stop()
        def conv_chunk(g, c, wi):
            ch = g * 4 + c
            i = proj_chunk(wi, c)
            cp("dve", U[:, c, 0:3], halo[:, ch, :], ["halo"], [("U", c)])
            cp("act", U[:, c, 3:515], PM[i][:], [PMK[i]], [("U", c)])
            yield
            ci = ch % 2
            acc = cacc[ci]
            ts("dve", acc[:], U[:, c, 0:512], convw[:, ch, 0:1], None, ALU.mult, None, [("U", c), "convw"], [("cacc", ci)])
            for tap in range(1, 4):
                stt("dve", acc[:], U[:, c, tap:tap + 512], convw[:, ch, tap:tap + 1], acc[:], ALU.mult, ALU.add,
                    [("U", c), "convw", ("cacc", ci)], [("cacc", ci)])
            cp(PC, halo[:, ch, :], U[:, c, 512:515], [("U", c)], ["halo"])
            act(etmp[ci][:], acc[:], AF.Tanh, [("cacc", ci)], [("etmp", ci)])
            yield
            yield
            stt("dve", QKV[:, ch, :], etmp[ci][:], 1.0, acc[:], ALU.add, ALU.mult, [("etmp", ci), ("cacc", ci)], [("QKV", ch)])

        for g in range(3):
            wi = load_wgroup(g)
            gens = []
            pend = [conv_chunk(g, c, wi) for c in range(4)]
            while gens or pend:
                if pend:
                    gens.append(pend.pop(0))
                for g_ in list(gens):
                    try:
                        next(g_)
                    except StopIteration:
                        gens.remove(g_)
        if stage == 3:
            raise _Stop()
        for g in (3, 5):
            wi = load_wgroup(g)
            for c in range(4):
                i = proj_chunk(wi, c)
                zc = (0 if g == 3 else 4) + c
                ei = c % 2
                act(etmp[ei][:], PM[i][:], AF.Tanh, [PMK[i]], [("etmp", ei)], scale=0.5)
                stt("dve", SZ[:, zc, :], etmp[ei][:], 1.0, PM[i][:], ALU.add, ALU.mult, [("etmp", ei), PMK[i]], [("SZ", zc)])
        pab = PN[:, 32:64].rearrange("p (b c) -> p b c", b=4)
        tt("dve", tmp16[:].rearrange("p (b h) -> p b h", b=4), pab[:, :, 0:4], dtb[:].rearrange("p (b h) -> p b h", b=4), ALU.add,
           ["PN", "dtb"], ["tmp16"])
        act(tmp16[:], tmp16[:], AF.Exp, ["tmp16"], ["tmp16"])
        act(tmp16[:], tmp16[:], AF.Ln, ["tmp16", "onec"], ["tmp16"], bias=onec[:])
        tt("dve", gtok[:], tmp16[:], negA[:], ALU.mult, ["tmp16", "negA"], ["gtok"])
        act(tmp16b[:].rearrange("p (b h) -> p b h", b=4), pab[:, :, 4:8], AF.Exp, ["PN"], ["tmp16b"], scale=-1.0)
        ts("dve", tmp16b[:], tmp16b[:], 1.0, None, ALU.add, None, ["tmp16b"], ["tmp16b"])
        recip(btok[:], tmp16b[:], ["tmp16b"], ["btok"])
        ts("dve", nbeta[:], btok[:], -1.0, None, ALU.mult, None, ["btok"], ["nbeta"])
        mm(PN[:, 64:80], trif[:], gtok[:], True, True, ["trif", "gtok"], ["PN"])
        mm(PN[:, 80:96], onesf[:], gtok[:], True, True, ["onesf", "gtok"], ["PN"])
        cp("dve", gc[:], PN[:, 64:80], ["PN"], ["gc"])
        cp("dve", gsum[:], PN[:, 80:96], ["PN"], ["gsum"])
        for b in range(4):
            mm(PM[0][0:4, b * 128:(b + 1) * 128], gtok[:, b * 4:(b + 1) * 4], trif[:], True, True, ["trif", "gtok"], [("PM", 0)])
        cp("dve", gcrow4[:], PM[0][0:4, :], [("PM", 0)], ["gcrow4"])
        act(egc[:], gc[:], AF.Exp, ["gc"], ["egc"])
        act(egl[:], gsum[:], AF.Exp, ["gsum"], ["egl"])
        tt("dve", ekd[:], gsum[:], gc[:], ALU.subtract, ["gsum", "gc"], ["ekd"])
        act(ekd[:], ekd[:], AF.Exp, ["ekd"], ["ekd"])
        tt("dve", bek[:], btok[:], egc[:], ALU.mult, ["btok", "egc"], ["bek"])

        wi4 = load_wgroup(4)

        def qk_src(ch):
            return lambda: (QKV[:, ch, :], [("QKV", ch)])

        def qf_src(c):
            def f():
                i = proj_chunk(wi4, c)
                return PM[i][:], [PMK[i]]
            return f

        run_staggered([norm_gen(qk_src(ch), QKV[:, ch, :], [("QKV", ch)], 1.0, (128.0 ** -0.5) if ch < 4 else 1.0, [])
                       for ch in range(8)]
                      + [norm_gen(qf_src(c), QT[:, c, :], [("QT", c)], 1.0 / 128, gqs[:], ["gqs"]) for c in range(4)])

        if stage == 4:
            raise _Stop()
        ce_ = "dve" if tt_ >= CE_T else "act"

        def gdn_head(h):
            GBh = GB[h]
            gk = GBk[h]
            kK, kQ, kV = ("QKV", 4 + h), ("QKV", h), ("QKV", 8 + h)
            K_ = lambda nm: (nm, h)
            for b in range(4):
                bc = slice(b * 128, (b + 1) * 128)
                r_ = b * 4 + h
                Ka = QKV[:, 4 + h, bc]
                Qa = QKV[:, h, bc]
                Va = QKV[:, 8 + h, bc]
                mm(GBh[:, 0, :], sel4[:, h * 128:(h + 1) * 128], gcrow4[:, bc], True, True, ["sel4", "gcrow4"], [gk])
                yield
                ts("dve", t1[h][:], GBh[:, 0, :], gc[:, r_:r_ + 1], 0.0, ALU.subtract, ALU.min, [gk, "gc"], [K_("t1")])
                ts("dve", t2[h][:], GBh[:, 0, :], gc[:, r_:r_ + 1], 0.0, ALU.subtract, ALU.max, [gk, "gc"], [K_("t2")])
                yield
                act(t1[h][:], t1[h][:], AF.Exp, [K_("t1")], [K_("t1")])
                act(t2[h][:], t2[h][:], AF.Exp, [K_("t2")], [K_("t2")], scale=-1.0)
                mm(GBh[:, 1, :], Ka, Ka, True, True, [kK], [gk])
                mm(GBh[:, 2, :], Ka, Qa, True, True, [kK, kQ], [gk])
                yield
                stt("dve", x1[h][:], GBh[:, 1, :], nbeta[:, r_:r_ + 1], t2[h][:], ALU.mult, ALU.mult, [gk, "nbeta", K_("t2")], [K_("x1")])
                tt("dve", pt1[h][:], GBh[:, 2, :], t1[h][:], ALU.mult, [gk, K_("t1")], [K_("pt1")])
                yield
                tt(PC, Xb[h][0][:], x1[h][:], ml0f[:], ALU.mult, [K_("x1"), "ml0f"], [("Xb", 0, h)])
                tt(PC, X1b[h][:], x1[h][:], ml1f[:], ALU.mult, [K_("x1"), "ml1f"], [K_("X1b")])
                tt(PC, X2b[h][:], x1[h][:], ml2f[:], ALU.mult, [K_("x1"), "ml2f"], [K_("X2b")])
                tt(PC, PTm[h][:], pt1[h][:], trif[:], ALU.mult, [K_("pt1"), "trif"], [K_("PTm")])
                yield
                mm(GBh[:, 3, :], Xb[h][0][:], identb[:], True, True, [("Xb", 0, h), "identb"], [gk])
                yield
                cp(ce_, Yb[h][0][:], GBh[:, 3, :], [gk], [("Yb", 0, h)])
                tt("dve", Rb[h][:], GBh[:, 3, :], identb[:], ALU.add, [gk, "identb"], [K_("Rb")])
                yield
                cur = 0
                for lv in range(1, 5):
                    nx = 1 - cur
                    if lv <= 3:
                        mm(GBh[:, 0, :], Xb[h][cur][:], Yb[h][cur][:], True, True, [("Xb", cur, h), ("Yb", cur, h)], [gk])
                    mm(GBh[:, 1, :], Yb[h][cur][:], Xb[h][cur][:], True, True, [("Xb", cur, h), ("Yb", cur, h)], [gk])
                    yield
                    if lv <= 3:
                        cp("dve", Yb[h][nx][:], GBh[:, 0, :], [gk], [("Yb", nx, h)])
                    cp(ce_, Xb[h][nx][:], GBh[:, 1, :], [gk], [("Xb", nx, h)])
                    yield
                    mm(GBh[:, 2, :], Xb[h][nx][:], Rb[h][:], True, True, [("Xb", nx, h), K_("Rb")], [gk])
                    yield
                    tt("dve", Rb[h][:], GBh[:, 2, :], Rb[h][:], ALU.add, [K_("Rb"), gk], [K_("Rb")])
                    yield
                    cur = nx
                for Xl, kX in ((X1b[h], K_("X1b")), (X2b[h], K_("X2b"))):
                    mm(GBh[:, 0, :], Rb[h][:], identb[:], True, True, [K_("Rb"), "identb"], [gk])
                    mm(GBh[:, 1, :], Xl[:], Rb[h][:], True, True, [kX, K_("Rb")], [gk])
                    yield
                    cp(ce_, Tb[h][:], GBh[:, 0, :], [gk], [K_("Tb")])
                    cp("dve", Wb[h][:], GBh[:, 1, :], [gk], [K_("Wb")])
                    yield
                    mm(GBh[:, 2, :], Tb[h][:], Wb[h][:], True, True, [K_("Tb"), K_("Wb")], [gk])
                    yield
                    tt("dve", Rb[h][:], GBh[:, 2, :], Rb[h][:], ALU.add, [K_("Rb"), gk], [K_("Rb")])
                    yield
                mm(GBh[:, 0, :], Va, identb[:], True, True, [kV, "identb"], [gk])
                mm(GBh[:, 1, :], Ka, identb[:], True, True, [kK, "identb"], [gk])
                yield
                ts("dve", Vb[h][:], GBh[:, 0, :], btok[:, r_:r_ + 1], None, ALU.mult, None, [gk, "btok"], [K_("Vb")])
                ts("dve", Kb[h][:], GBh[:, 1, :], bek[:, r_:r_ + 1], None, ALU.mult, None, [gk, "bek"], [K_("Kb")])
                if ce_ == "dve":
                    ts("dve", Kd[h][:], GBh[:, 1, :], ekd[:, r_:r_ + 1], None, ALU.mult, None, [gk, "ekd"], [K_("Kd")])
                else:
                    act(Kd[h][:], GBh[:, 1, :], AF.Identity, [gk, "ekd"], [K_("Kd")], scale=ekd[:, r_:r_ + 1])
                yield
                mm(GBh[:, 2, :], Rb[h][:], Vb[h][:], True, True, [K_("Rb"), K_("Vb")], [gk])
                mm(GBh[:, 3, :], Kb[h][:], Rb[h][:], True, True, [K_("Rb"), K_("Kb")], [gk])
                yield
                cp(ce_, Uf[h][:], GBh[:, 2, :], [gk], [K_("Uf")])
                cp("dve", WT[h][:], GBh[:, 3, :], [gk], [K_("WT")])
                yield
                mm(GBh[:, 0, :], WT[h][:], Sb[h][:], True, True, [K_("WT"), ("Sb", h)], [gk])
                mm(GBh[:, 1, :], Qa, Sb[h][:], True, True, [kQ, ("Sb", h)], [gk])
                yield
                stt("dve", vnew[h][:], GBh[:, 0, :], -1.0, Uf[h][:], ALU.mult, ALU.add, [K_("Uf"), gk], [K_("vnew")])
                if ce_ == "dve":
                    ts("dve", Of[h][:], GBh[:, 1, :], egc[:, r_:r_ + 1], None, ALU.mult, None, [gk, "egc"], [K_("Of")])
                else:
                    act(Of[h][:], GBh[:, 1, :], AF.Identity, [gk, "egc"], [K_("Of")], scale=egc[:, r_:r_ + 1])
                yield
                mm(GBh[:, 2, :], PTm[h][:], vnew[h][:], True, True, [K_("PTm"), K_("vnew")], [gk])
                mm(GBh[:, 3, :], Kd[h][:], vnew[h][:], True, True, [K_("Kd"), K_("vnew")], [gk])
                yield
                tt("dve", Of[h][:], GBh[:, 2, :], Of[h][:], ALU.add, [K_("Of"), gk], [K_("Of")])
                ts("dve", Sf[h][:], Sf[h][:], egl[:, r_:r_ + 1], None, ALU.mult, None, [("Sf", h), "egl"], [("Sf", h)])
                tt("dve", Sf[h][:], GBh[:, 3, :], Sf[h][:], ALU.add, [("Sf", h), gk], [("Sf", h)])
                yield
                cp("pool" if ce_ == "dve" else "act", Sb[h][:], Sf[h][:], [("Sf", h)], [("Sb", h)])
                mset("dve", ms[h][:], 0.0, [K_("ms")])
                act(junk128[:], Of[h][:], AF.Square, [K_("Of"), K_("ms")], ["junk128", K_("ms")], scale=128.0 ** -0.5, accum=ms[h][:])
                yield
                act(ms[h][:], ms[h][:], AF.Ln, [K_("ms"), "epsc"], [K_("ms")], bias=epsc[:])
                act(ms[h][:], ms[h][:], AF.Exp, [K_("ms")], [K_("ms")], scale=-0.5)
                yield
                ts("dve", onb[h][:], Of[h][:], ms[h][:], None, ALU.mult, None, [K_("Of"), K_("ms")], [K_("onb")])
                yield
                mm(GBh[:, 0, :], onb[h][:], identb[:], True, True, [K_("onb"), "identb"], [gk])
                yield
                stt("dve", oaT[:, h, bc], GBh[:, 0, :], ggdn[:], SZ[:, h, bc], ALU.mult, ALU.mult,
                    [gk, "ggdn", ("SZ", h)], [("oaT", h)])
                yield

        nkb = 4 * tt_ + 4
        noff = 4 * tt_

        def attn_gen():
            for h in range(4):
                for b in range(4):
                    col = (4 * tt_ + b) * 4 + h
                    ts("dve", Zz[:, b, b:b + 1], CTOK[:, col:col + 1], BASE[:, 16 * tt_ + h:16 * tt_ + h + 1], None, ALU.subtract, None,
                       ["CTOK", "BASE"], ["Zz"])
                for b in range(4):
                    mm(PN[0:4, b * 128:(b + 1) * 128], Zz[:, b, :], identf[:], True, True, ["Zz", "identf"], ["PN"])
                cp("dve", crx[0:4, :], PN[0:4, :], ["PN"], ["crx"])
                tt("dve", crh[:], PN[0:4, :], crx[0:4, :], ALU.subtract, ["PN", "crx"], ["crh"])
                cp("dve", crx[32:36, :], crh[:], ["crh"], ["crx"])
                P.add("dve", lambda e, h=h: e.tensor_scalar(
                    out=BT[:, 0:nkb], in0=CTOK[:, 0:nkb * 4].rearrange("p (k h) -> p k h", h=4)[:, :, h], scalar1=-1.0,
                    scalar2=REFM[:, 16 * tt_ + h:16 * tt_ + h + 1], op0=ALU.mult, op1=ALU.add), ["CTOK", "REFM"], ["BT"])
                yield
                if noff:
                    mm(PN[:], ones36[:], crx[:], True, True, ["ones36", "crx"], ["PN"])
                    act(ma[:], PN[:], AF.Exp, ["PN"], ["ma"])
                    yield
                blocks = []
                for kt in range(tt_ + 1):
                    for j in range(4):
                        blocks.append((kt, j))
                kvslot = {}

                def issue_kv(kt):
                    if kt > tt_ or kt in kvslot:
                        return
                    si_ = kt % 2
                    kvslot[kt] = si_
                    dma("sp", kst[si_][:], KT_d[h, :, kt * 512:(kt + 1) * 512], ["KT_d"], [("kst", si_)])
                    dma("sp", vst[si_][:], V1_d[kt * 4:(kt + 1) * 4].rearrange("b p n -> p b n"), ["V1_d"], [("vst", si_)])

                def emit_scores(bi):
                    kt, j = blocks[bi]
                    si_ = kvslot[kt]
                    diag = kt == tt_
                    c0 = j * 128 if diag else 0
                    i = bi % 2
                    mm(PM[i][:, c0:512], kst[si_][:, j * 128:(j + 1) * 128], QT[:, h, c0:512], True, not diag,
                       [("kst", si_), ("QT", h)], [PMK[i]])
                    if diag:
                        mm(PM[i][:, c0:512], ones36[:], crx[:, c0:512], False, True, ["ones36", "crx"], [PMK[i]])

                def emit_rest(bi):
                    kt, j = blocks[bi]
                    si_ = kvslot[kt]
                    kb = kt * 4 + j
                    diag = kt == tt_
                    c0 = j * 128 if diag else 0
                    i = bi % 2
                    pi = bi % 2
                    act(ptt[pi][:, c0:512], PM[i][:, c0:512], AF.Exp, [PMK[i], "BT"], [("ptt", pi)], bias=BT[:, kb:kb + 1])
                    if diag:
                        tt(PC, ptt[pi][:, c0:c0 + 128], ptt[pi][:, c0:c0 + 128], trib[:], ALU.mult, [("ptt", pi), "trib"], [("ptt", pi)])
                        st_, sp_ = (j == 0), (j == 3)
                    else:
                        st_, sp_ = (kb == 0), (kb == noff - 1)
                    mm(POT[:, c0:512], vst[si_][:, j, h * 129:h * 129 + 128], ptt[pi][:, c0:512], st_, sp_,
                       [("ptt", pi), ("vst", si_)], ["POT"])
                    mm(PN[:, c0:512], onesb[:], ptt[pi][:, c0:512], st_, sp_, [("ptt", pi), "onesb"], ["PN"])
                    if (not diag) and kb == noff - 1:
                        tt("dve", cacc[0][:], POT[:], ma[:], ALU.mult, ["POT", "ma"], [("cacc", 0)])
                        tt("dve", cacc[1][:], PN[:], ma[:], ALU.mult, ["PN", "ma"], [("cacc", 1)])

                issue_kv(0)
                issue_kv(1)
                emit_scores(0)
                for bi in range(len(blocks)):
                    if bi + 1 < len(blocks):
                        emit_scores(bi + 1)
                    emit_rest(bi)
                    if blocks[bi][1] == 3:
                        issue_kv(blocks[bi][0] + 2)
                    yield
                if noff:
                    tt("dve", etmp[1][:], PN[:], cacc[1][:], ALU.add, ["PN", ("cacc", 1)], [("etmp", 1)])
                    tt("dve", etmp[0][:], POT[:], cacc[0][:], ALU.add, ["POT", ("cacc", 0)], [("etmp", 0)])
                else:
                    cp("dve", etmp[1][:], PN[:], ["PN"], [("etmp", 1)])
                    cp("dve", etmp[0][:], POT[:], ["POT"], [("etmp", 0)])
                yield
                recip(etmp[1][:], etmp[1][:], [("etmp", 1)], [("etmp", 1)])
                tt("dve", etmp[0][:], etmp[0][:], etmp[1][:], ALU.mult, [("etmp", 0), ("etmp", 1)], [("etmp", 0)])
                tt("dve", ofT[:, h, :], etmp[0][:], SZ[:, 4 + h, :], ALU.mult, [("etmp", 0), ("SZ", 4 + h)], [("ofT", h)])
                yield

        gens = [gdn_head(h) for h in range(4)]
        ag = attn_gen()
        ratio = (12.0 + 16.0 * (tt_ + 1)) / 176.0
        accr = 0.0
        ag_live = True
        while gens:
            for g_ in list(gens):
                try:
                    next(g_)
                except StopIteration:
                    gens.remove(g_)
            accr += ratio
            while ag_live and accr >= 1.0:
                accr -= 1.0
                try:
                    next(ag)
                except StopIteration:
                    ag_live = False
        while ag_live:
            try:
                next(ag)
            except StopIteration:
                ag_live = False

        if stage == 6:
            raise _Stop()
        for g4 in range(4):
            wi = load_wgroup(6 + g4)
            for c in range(4):
                i = proj_chunk(wi, c)
                act(SG[:, c, :], PM[i][:], AF.Tanh, [PMK[i]], [("SG", c)], scale=0.5)
            for c2 in range(2):
                n = g4 * 2 + c2
                i = next_pm()
                for h in range(4):
                    mm(PM[i][:], wogb[:, h, n * 128:(n + 1) * 128], oaT[:, h, :], h == 0, h == 3, ["wogb", ("oaT", h)], [PMK[i]])
                stt("dve", ma[:], SG[:, c2, :], 1.0, PM[i][:], ALU.add, ALU.mult, [PMK[i], ("SG", c2)], ["ma"])
                i = next_pm()
                for h in range(4):
                    mm(PM[i][:], wofb[:, h, n * 128:(n + 1) * 128], ofT[:, h, :], h == 0, h == 3, ["wofb", ("ofT", h)], [PMK[i]])
                stt("dve", etmp[0][:], SG[:, 2 + c2, :], 1.0, PM[i][:], ALU.add, ALU.mult, [PMK[i], ("SG", 2 + c2)], [("etmp", 0)])
                tt("dve", mT[:, n, :], etmp[0][:], ma[:], ALU.add, [("etmp", 0), "ma"], [("mT", n)])
        if stage == 7:
            raise _Stop()
        for b in range(4):
            s_i = xi[0] % 2
            xi[0] += 1
            xb = stg[s_i]
            dma("sp", xb[:], x_d[t0 + b * 128:t0 + (b + 1) * 128, :], [], [("stg", s_i)])
            for n2 in range(2):
                i = next_pm()
                for k in range(8):
                    mm(PM[i][:], mT[:, k, b * 128:(b + 1) * 128], woutb[:, k, n2 * 512:(n2 + 1) * 512], k == 0, k == 7,
                       [("mT", k), "woutb"], [PMK[i]])
                tt("dve", xb[:, n2 * 512:(n2 + 1) * 512], PM[i][:], xb[:, n2 * 512:(n2 + 1) * 512], ALU.add,
                   [PMK[i], ("stg", s_i)], [("stg", s_i)])
            dma("pool", out_d[t0 + b * 128:t0 + (b + 1) * 128, :], xb[:], [("stg", s_i)], [("out", tt_, b)])

    try:
        for tt_ in range(NT):
            passB_tile(tt_)
    except _Stop:
        pass
    P.emit()
    st.close()
    return nc, P.n_ops


def host_inputs(inp, b):
    f = np.float32
    w_in = np.asarray(inp["w_in"], f)
    o = np.cumsum((0, 512, 512, 512, 4, 4, 512, 512, 512, 512, 4, 512, 1024, 1024))
    qa, ka, va, a_a, b_a, za, qf, kf, vf, f_f, zf, ga, gf = [w_in[:, o[i]:o[i + 1]] for i in range(13)]
    pairs = []
    for g in range(4):
        pairs += [ga[:, 2 * g * 128:(2 * g + 2) * 128], gf[:, 2 * g * 128:(2 * g + 2) * 128]]
    wB = np.ascontiguousarray(np.concatenate([qa, ka, va, za, qf, zf] + pairs, axis=1))
    c = np.asarray(inp["c"], f)[b]
    b_ada = np.asarray(inp["b_ada"], f)
    til = lambda v: np.ascontiguousarray(np.broadcast_to(np.tile(np.asarray(v, f), 4)[None, :], (128, 16)))
    col = lambda v: np.ascontiguousarray(np.asarray(v, f).reshape(128, 1))
    idx = np.arange(128)
    sel = np.zeros((16, 16, 128), f)
    for r in range(16):
        sel[r, r, :] = 1
    return {
        "x": np.ascontiguousarray(np.asarray(inp["x"], f)[b]),
        "cT": np.ascontiguousarray(c.reshape(8, 128).T),
        "w_ada": np.ascontiguousarray(np.asarray(inp["w_ada"], f)),
        "b_adaT": np.ascontiguousarray(b_ada[:2048].reshape(16, 128).T),
        "b_gate": np.ascontiguousarray(np.broadcast_to(b_ada[None, 2048:], (128, 1024))),
        "g_normT": np.ascontiguousarray(np.asarray(inp["g_norm"], f).reshape(8, 128).T),
        "wAk": np.ascontiguousarray(kf), "wAv": np.ascontiguousarray(vf), "wAf": np.ascontiguousarray(f_f),
        "wB": wB, "wab": np.ascontiguousarray(np.concatenate([a_a, b_a], axis=1)),
        "conv_wT": np.ascontiguousarray(np.asarray(inp["conv_w"], f).T.reshape(12, 128, 4).transpose(1, 0, 2)),
        "alog_bc": til(inp["A_log"]), "dtb_bc": til(inp["dt_bias"]), "bf_bc": til(inp["b_f"]),
        "g_gdn": col(inp["g_gdn_out"]), "g_q": col(inp["g_q_fox"]), "g_k": col(inp["g_k_fox"]),
        "g_q_row": np.ascontiguousarray(np.asarray(inp["g_q_fox"], f).reshape(1, 128)),
        "g_k_row": np.ascontiguousarray(np.asarray(inp["g_k_fox"], f).reshape(1, 128)),
        "w_o_gdn": np.ascontiguousarray(np.asarray(inp["w_o_gdn"], f)),
        "w_o_fox": np.ascontiguousarray(np.asarray(inp["w_o_fox"], f)),
        "w_out": np.ascontiguousarray(np.asarray(inp["w_out"], f)),
        "ident": np.eye(128, dtype=f),
        "tri": (idx[:, None] <= idx[None, :]).astype(f),
        "ml0": ((idx[None, :] < idx[:, None]) & (idx[None, :] // 32 == idx[:, None] // 32)).astype(f),
        "ml1": ((idx[None, :] < idx[:, None]) & (idx[None, :] // 32 != idx[:, None] // 32)
                & (idx[None, :] // 64 == idx[:, None] // 64)).astype(f),
        "ml2": ((idx[None, :] < idx[:, None]) & (idx[None, :] // 64 != idx[:, None] // 64)).astype(f),
        "sel4": np.ascontiguousarray(np.kron(np.eye(4, dtype=f), np.ones((1, 128), f))),
    }


def kernel(**inputs):
    x = np.asarray(inputs["x"])
    B, T, _ = x.shape
    nc, _ = build(T)
    in_maps = [host_inputs(inputs, b) for b in range(B)]
    res = run_bass_kernel_spmd(nc, in_maps, core_ids=list(range(B)))
    return np.stack([np.asarray(r["out"], np.float32) for r in res.results], axis=0)
```
